# Optimizing a Trainium2 kernel written in Bass

```python
import math
import jax, jax.numpy as jnp
from jax import lax
import numpy as np

D_MODEL = 1024
BATCH = 8
SEQ = 4096
DEPTH = 1

CTX_LEN = 256
GRID_W = 64
MIX_WIDTH = 2 * D_MODEL
CONV_WIDTH = MIX_WIDTH // 2
SSM_WIDTH = MIX_WIDTH - CONV_WIDTH
CONV_TAPS = 31
CONV_PAD = CONV_TAPS // 2
SSM_GROUP = 16
SSM_GROUPS = SSM_WIDTH // SSM_GROUP
SSM_STATE = 64
DT_MIN = 1e-3
DT_MAX = 1e-1
EPS = 1e-6
IN_COLS = 3 * CONV_WIDTH + 2 * SSM_WIDTH
SPLITS = (CONV_WIDTH, 2 * CONV_WIDTH, 3 * CONV_WIDTH, 3 * CONV_WIDTH + SSM_WIDTH)
U_START = 3 * CONV_WIDTH

kernel_name = 'hybrid_conformer_s5_prefix_dit_layer'


def rmsnorm(x, g):
    xf = x.astype(jnp.float32)
    y = xf * lax.rsqrt(jnp.mean(xf * xf, axis=-1, keepdims=True) + EPS)
    return (y * g.astype(jnp.float32)).astype(x.dtype)


def layernorm(x, g, b):
    xf = x.astype(jnp.float32)
    mu = jnp.mean(xf, axis=-1, keepdims=True)
    var = jnp.mean(jnp.square(xf - mu), axis=-1, keepdims=True)
    y = (xf - mu) * lax.rsqrt(var + EPS)
    return (y * g.astype(jnp.float32) + b.astype(jnp.float32)).astype(x.dtype)


def dwconv_latent(v, w, bias):
    bsz, length, ch = v.shape
    rows = length // GRID_W
    half = ch // 2
    vg = v.reshape(bsz, rows, GRID_W, ch)
    dn = ('NHWC', 'HWIO', 'NHWC')
    w_h = w[:, :half].reshape(1, CONV_TAPS, 1, half)
    w_v = w[:, half:].reshape(CONV_TAPS, 1, 1, ch - half)
    y_h = lax.conv_general_dilated(vg[..., :half], w_h, (1, 1), ((0, 0), (CONV_PAD, CONV_PAD)),
                                   dimension_numbers=dn, feature_group_count=half)
    y_v = lax.conv_general_dilated(vg[..., half:], w_v, (1, 1), ((CONV_PAD, CONV_PAD), (0, 0)),
                                   dimension_numbers=dn, feature_group_count=ch - half)
    return jnp.concatenate([y_h, y_v], axis=-1).reshape(bsz, length, ch) + bias


def dwconv_seq(v, w, bias):
    ch = v.shape[-1]
    y = lax.conv_general_dilated(v, w.reshape(CONV_TAPS, 1, ch), (1,), ((CONV_PAD, CONV_PAD),),
                                 dimension_numbers=('NWC', 'WIO', 'NWC'), feature_group_count=ch)
    return y + bias


def conv_branch(val, glu_gate, silu_gate, w_dw, b_dw, ln_g, ln_b, on_grid):
    v = val * jax.nn.sigmoid(glu_gate)
    v = dwconv_latent(v, w_dw, b_dw) if on_grid else dwconv_seq(v, w_dw, b_dw)
    v = jax.nn.silu(layernorm(v, ln_g, ln_b))
    return v * jax.nn.silu(silu_gate)


def s5_discretise(a_re, a_im, log_dt, b_re, b_im):
    lam = lax.complex(a_re.astype(jnp.float32), a_im.astype(jnp.float32))
    dt = jnp.exp(log_dt.astype(jnp.float32))[:, None]
    lam_bar = jnp.exp(lam * dt)
    b_mat = lax.complex(b_re.astype(jnp.float32), b_im.astype(jnp.float32))
    b_bar = ((lam_bar - 1.0) / lam)[..., None] * b_mat
    return lam_bar, b_bar


def _linear_recurrence(e1, e2):
    a1, b1 = e1
    a2, b2 = e2
    return a1 * a2, a2 * b1 + b2


def s5_direction(u4, lam_bar, b_bar, h0, reverse):
    bu = jnp.einsum('blgh,gph->lbgp', u4.astype(jnp.complex64), b_bar)
    if reverse:
        bu = bu[::-1]
    if h0 is not None:
        bu = bu.at[0].add(lam_bar * h0)
    a = jnp.broadcast_to(lam_bar, (bu.shape[0], 1) + lam_bar.shape)
    _, h = lax.associative_scan(_linear_recurrence, (a, bu), axis=0)
    final = h[-1]
    if reverse:
        h = h[::-1]
    return h, final


def s5_readout(h, c_mat):
    return jnp.real(jnp.einsum('lbgp,ghp->blgh', h, c_mat))


def ssm_branch(u, y_f, y_b, silu_gate, d, glu_w, glu_b):
    bsz, length, _ = u.shape
    y = (y_f + y_b).reshape(bsz, length, SSM_WIDTH) + d.astype(jnp.float32) * u.astype(jnp.float32)
    y = jax.nn.gelu(y).astype(u.dtype)
    y = y * jax.nn.sigmoid(y @ glu_w + glu_b)
    return y * jax.nn.silu(silu_gate)


def setup_inputs(seed: int = 0) -> dict:
    key = jax.random.key(seed)
    ks = jax.random.split(key, 24)
    f32 = jnp.float32
    nrm = lambda k, shape, s: jax.random.normal(k, shape, f32) * s
    n_idx = jnp.arange(SSM_STATE, dtype=f32)
    return {
        'x': nrm(ks[0], (BATCH, SEQ, D_MODEL), 1.0),
        'c': nrm(ks[1], (BATCH, D_MODEL), 1.0),
        'ctx': nrm(ks[2], (BATCH, CTX_LEN, D_MODEL), 1.0),
        'c_ctx': nrm(ks[3], (D_MODEL,), 1.0),
        'norm_g': 1.0 + nrm(ks[4], (DEPTH, D_MODEL), 0.02),
        'w_ada': nrm(ks[5], (DEPTH, D_MODEL, 3 * D_MODEL), D_MODEL ** -0.5),
        'b_ada': nrm(ks[6], (DEPTH, 3 * D_MODEL), 0.02),
        'w_in': nrm(ks[7], (DEPTH, D_MODEL, IN_COLS), D_MODEL ** -0.5),
        'conv_dw': nrm(ks[8], (DEPTH, CONV_TAPS, CONV_WIDTH), CONV_TAPS ** -0.5),
        'conv_db': nrm(ks[9], (DEPTH, CONV_WIDTH), 0.02),
        'conv_ln_g': 1.0 + nrm(ks[10], (DEPTH, CONV_WIDTH), 0.02),
        'conv_ln_b': nrm(ks[11], (DEPTH, CONV_WIDTH), 0.02),
        'ssm_a_re': -0.5 + nrm(ks[12], (DEPTH, 2, SSM_GROUPS, SSM_STATE), 0.01),
        'ssm_a_im': math.pi * n_idx + nrm(ks[13], (DEPTH, 2, SSM_GROUPS, SSM_STATE), 0.01),
        'ssm_log_dt': jax.random.uniform(ks[14], (DEPTH, 2, SSM_GROUPS), f32,
                                         math.log(DT_MIN), math.log(DT_MAX)),
        'ssm_b_re': nrm(ks[15], (DEPTH, 2, SSM_GROUPS, SSM_STATE, SSM_GROUP), (2 * SSM_GROUP) ** -0.5),
        'ssm_b_im': nrm(ks[16], (DEPTH, 2, SSM_GROUPS, SSM_STATE, SSM_GROUP), (2 * SSM_GROUP) ** -0.5),
        'ssm_c_re': nrm(ks[17], (DEPTH, 2, SSM_GROUPS, SSM_GROUP, SSM_STATE), SSM_STATE ** -0.5),
        'ssm_c_im': nrm(ks[18], (DEPTH, 2, SSM_GROUPS, SSM_GROUP, SSM_STATE), SSM_STATE ** -0.5),
        'ssm_d': nrm(ks[19], (DEPTH, SSM_WIDTH), 1.0),
        'ssm_glu_w': nrm(ks[20], (DEPTH, SSM_WIDTH, SSM_WIDTH), SSM_WIDTH ** -0.5),
        'ssm_glu_b': nrm(ks[21], (DEPTH, SSM_WIDTH), 0.02),
        'w_out': nrm(ks[22], (DEPTH, MIX_WIDTH, D_MODEL), MIX_WIDTH ** -0.5),
        'final_g': 1.0 + nrm(ks[23], (D_MODEL,), 0.02),
    }


def reference(x, c, ctx, c_ctx, norm_g, w_ada, b_ada, w_in, conv_dw, conv_db, conv_ln_g, conv_ln_b,
              ssm_a_re, ssm_a_im, ssm_log_dt, ssm_b_re, ssm_b_im, ssm_c_re, ssm_c_im, ssm_d,
              ssm_glu_w, ssm_glu_b, w_out, final_g):
    bsz, length, _ = x.shape
    ctx_len = ctx.shape[1]
    s_c = jax.nn.silu(c)
    s_cc = jax.nn.silu(c_ctx)
    xc = ctx
    for i in range(DEPTH):
        last = i == DEPTH - 1
        shift, scale, gate = jnp.split(s_c @ w_ada[i] + b_ada[i], 3, axis=-1)
        shift_c, scale_c, gate_c = jnp.split(s_cc @ w_ada[i] + b_ada[i], 3, axis=-1)
        h = rmsnorm(x, norm_g[i]) * (1.0 + scale[:, None]) + shift[:, None]
        hc = rmsnorm(xc, norm_g[i]) * (1.0 + scale_c) + shift_c

        lam_f, bbar_f = s5_discretise(ssm_a_re[i, 0], ssm_a_im[i, 0], ssm_log_dt[i, 0], ssm_b_re[i, 0], ssm_b_im[i, 0])
        lam_b, bbar_b = s5_discretise(ssm_a_re[i, 1], ssm_a_im[i, 1], ssm_log_dt[i, 1], ssm_b_re[i, 1], ssm_b_im[i, 1])
        cmat_f = lax.complex(ssm_c_re[i, 0].astype(jnp.float32), ssm_c_im[i, 0].astype(jnp.float32))
        cmat_b = lax.complex(ssm_c_re[i, 1].astype(jnp.float32), ssm_c_im[i, 1].astype(jnp.float32))

        if last:
            u_c = hc @ w_in[i][:, U_START:U_START + SSM_WIDTH]
        else:
            p_c = jnp.split(hc @ w_in[i], SPLITS, axis=-1)
            u_c = p_c[3]
        u_c4 = u_c.astype(jnp.float32).reshape(bsz, ctx_len, SSM_GROUPS, SSM_GROUP)
        hc_f, fin_f = s5_direction(u_c4, lam_f, bbar_f, None, False)
        hc_b, fin_b = s5_direction(u_c4, lam_b, bbar_b, None, True)

        p = jnp.split(h @ w_in[i], SPLITS, axis=-1)
        u4 = p[3].astype(jnp.float32).reshape(bsz, length, SSM_GROUPS, SSM_GROUP)
        hf, _ = s5_direction(u4, lam_f, bbar_f, fin_f, False)
        y_f = s5_readout(hf, cmat_f)
        hb, _ = s5_direction(u4, lam_b, bbar_b, fin_b, True)
        y_b = s5_readout(hb, cmat_b)
        conv_out = conv_branch(p[0], p[1], p[2], conv_dw[i], conv_db[i], conv_ln_g[i], conv_ln_b[i], True)
        ssm_out = ssm_branch(p[3], y_f, y_b, p[4], ssm_d[i], ssm_glu_w[i], ssm_glu_b[i])
        mix = jnp.concatenate([conv_out, ssm_out], axis=-1) @ w_out[i]
        x = x + gate[:, None] * mix

        if not last:
            conv_c = conv_branch(p_c[0], p_c[1], p_c[2], conv_dw[i], conv_db[i], conv_ln_g[i], conv_ln_b[i], False)
            ssm_c = ssm_branch(p_c[3], s5_readout(hc_f, cmat_f), s5_readout(hc_b, cmat_b), p_c[4],
                               ssm_d[i], ssm_glu_w[i], ssm_glu_b[i])
            xc = xc + gate_c * (jnp.concatenate([conv_c, ssm_c], axis=-1) @ w_out[i])
    return rmsnorm(x, final_g)
```

```python
import math
import numpy as np
import concourse.bass as bass
import concourse.mybir as mybir
from concourse.bass_utils import run_bass_kernel_spmd

F32 = mybir.dt.float32
BF16 = mybir.dt.bfloat16
I32 = mybir.dt.int32
ALU = mybir.AluOpType
AF = mybir.ActivationFunctionType

D = 1024
EPS = 1e-6
TWO_PI = 2.0 * math.pi


class Slot:
    __slots__ = ("name", "last_w", "readers", "excl")

    def __init__(self, name, excl=False):
        self.name = name
        self.last_w = None
        self.readers = []
        self.excl = excl


class Op:
    __slots__ = ("eng", "fn", "deps", "idx", "dma", "sem", "val", "needed", "prewait")

    def __init__(self, eng, fn, dma):
        self.eng = eng
        self.fn = fn
        self.dma = dma
        self.deps = set()
        self.sem = None
        self.val = None
        self.needed = False
        self.prewait = None


class Prog:
    ENGS = ("sync", "scalar", "vector", "gpsimd", "tensor")

    def __init__(self):
        self.ops = []

    limit = None
    _cap = None

    def capture(self):
        self._cap = []
        return self._cap

    def end_capture(self):
        c = self._cap
        self._cap = None
        return c

    def add_merged(self, *lists):
        its = [list(l) for l in lists]
        while any(its):
            for l in its:
                if l:
                    a = l.pop(0)
                    self.add(*a[:2], reads=a[2], writes=a[3], dma=a[4])

    def add(self, eng, fn, reads=(), writes=(), dma=False, force=False):
        if self._cap is not None:
            self._cap.append((eng, fn, list(reads), list(writes), dma))
            return None
        op = Op(eng, fn, dma)
        if self.limit is not None and len(self.ops) >= self.limit and not force:
            op.idx = -1
            return op
        op.idx = len(self.ops)
        for s in reads:
            if s.last_w is not None:
                op.deps.add(s.last_w)
            if s.excl:
                op.deps.update(s.readers)
        for s in writes:
            if s.last_w is not None:
                op.deps.add(s.last_w)
            op.deps.update(s.readers)
        for s in reads:
            s.readers.append(op.idx)
        for s in writes:
            s.last_w = op.idx
            s.readers = []
        op.deps.discard(op.idx)
        self.ops.append(op)
        return op

    def emit(self, nc, n_dma_sems=24):
        ops = self.ops
        for op in ops:
            if op.eng == "tensor" and not op.dma:
                op.deps = {d for d in op.deps if not (ops[d].eng == "tensor" and not ops[d].dma)}
            for d in op.deps:
                ops[d].needed = True
        import contextlib

        with contextlib.ExitStack() as es:
            esem = {e: es.enter_context(nc.semaphore("c_" + e)) for e in self.ENGS}
            dsem = {
                e: [es.enter_context(nc.semaphore("d_%s%d" % (e, i))) for i in range(n_dma_sems)]
                for e in ("sync", "gpsimd", "scalar")
            }
            ecount = {e: 0 for e in self.ENGS}
            dcount = {e: 0 for e in dsem}
            for op in ops:
                if op.dma:
                    i = dcount[op.eng]
                    dcount[op.eng] += 1
                    P = len(dsem[op.eng])
                    op.sem = dsem[op.eng][i % P]
                    op.val = 16 * (i // P + 1)
                    op.prewait = (op.sem, 16 * (i // P)) if i >= P else None
                elif op.needed:
                    ecount[op.eng] += 1
                    op.sem = esem[op.eng]
                    op.val = ecount[op.eng]
            block = es.enter_context(nc.Block())

            def make(ename):
                def body(eng):
                    waited = {}
                    for op in ops:
                        if op.eng != ename:
                            continue
                        wl = []
                        if op.prewait is not None:
                            wl.append(op.prewait)
                        for d in sorted(op.deps):
                            wl.append((ops[d].sem, ops[d].val))
                        for sem, val in wl:
                            k = id(sem)
                            if waited.get(k, 0) >= val:
                                continue
                            waited[k] = val
                            eng.wait_ge(sem, val)
                        ins = op.fn(eng)
                        if op.dma:
                            ins.then_inc(op.sem, 16)
                        elif op.needed:
                            ins.then_inc(op.sem, 1)

                return body

            for e in self.ENGS:
                getattr(block, e)(make(e))


def bcast(ap, n, axis=None):
    dims = [list(d) for d in ap.ap]
    if axis is None:
        dims.append([0, n])
    else:
        dims.insert(axis, [0, n])
    return bass.AP(ap.tensor, ap.offset, dims)


def rawap(ap, dims):
    return bass.AP(ap.tensor, ap.offset, [list(ap.ap[0])] + [list(d) for d in dims])


PV_C, PV_CC, PV_NG, PV_CDB, PV_LNG, PV_LNB, PV_SD, PV_GLB = [8 * i for i in range(8)]
PV_BADA = 64
PV_DW = 88
PV_N = 88 + 248


def build(nc, L=4096, CTX=256, stage=99, dbg=()):
    import contextlib

    NT = L // 512
    NCH = L // 8
    CCH = CTX // 8
    TOK = CTX + L
    P = Prog()
    import os
    if os.environ.get('OPLIMIT'):
        P.limit = int(os.environ['OPLIMIT'])
    es = contextlib.ExitStack()
    dram = lambda n, s, d=F32, k="ExternalInput": nc.dram_tensor(n, s, d, kind=k).ap()
    x_d = dram("x", [L, D])
    ctx_d = dram("ctx", [CTX, D])
    pv_d = dram("pv", [128, PV_N])
    idf_d = dram("ident", [128, 128])
    wada_d = dram("w_ada", [D, 3 * D])
    win_d = dram("w_in", [D, 5 * D])
    y_d = dram("y", [L, D], F32, "ExternalOutput")

    sb = lambda n, s, d=F32: es.enter_context(nc.sbuf_tensor("sb_" + n, s, d))
    pst = lambda n, s, d=F32: es.enter_context(nc.psum_tensor("ps_" + n, s, d))
    S = lambda n: Slot(n)

    pv = sb("pv", [128, PV_N]); s_pv = S("pv")
    idf = sb("idf", [128, 128]); s_idf = S("idf")
    idb = sb("idb", [128, 128], BF16); s_idb = S("idb")
    P.add("sync", lambda e: e.dma_start(out=pv[:], in_=pv_d[:, :]), writes=[s_pv], dma=True)
    P.add("sync", lambda e: e.dma_start(out=idf[:], in_=idf_d[:, :]), writes=[s_idf], dma=True)
    P.add("vector", lambda e: e.tensor_copy(out=idb[:], in_=idf[:]), reads=[s_idf], writes=[s_idb])

    sil = sb("sil", [128, 8, 2]); s_sil = S("sil")
    P.add("scalar", lambda e: e.activation(out=sil[:, :, 0], in_=pv[:, PV_C:PV_C + 8], func=AF.Silu),
          reads=[s_pv], writes=[s_sil])
    P.add("scalar", lambda e: e.activation(out=sil[:, :, 1], in_=pv[:, PV_CC:PV_CC + 8], func=AF.Silu),
          reads=[s_pv, s_sil], writes=[s_sil])
    vpad = sb("vpad", [128, 8192], BF16); s_v = S("vpad")
    wa_v = vpad[:].bitcast(F32)
    wa = [wa_v[:, i * 2048:(i + 1) * 2048].rearrange("p (k n) -> p k n", n=256) for i in range(2)]
    s_wa = [S("wa%d" % i) for i in range(2)]
    psA = [pst("psA%d" % i, [128, 512]) for i in range(2)]; s_psA = [Slot("psA%d" % i, True) for i in range(2)]
    psB = [pst("psB%d" % i, [128, 512]) for i in range(2)]; s_psB = [Slot("psB%d" % i, True) for i in range(2)]
    ps_ada = psB[1][:, 0:48]; s_psada = s_psB[1]
    wada_v = wada_d.rearrange("(kt p) n -> p kt n", p=128)
    for ch in range(12):
        b = ch % 2
        P.add("sync", lambda e, b=b, ch=ch: e.dma_start(out=wa[b], in_=wada_v[:, :, ch * 256:(ch + 1) * 256]),
              writes=[s_wa[b]], dma=True)

        def mm(e, b=b, ch=ch):
            ins = None
            for m4 in range(2):
                m = ch * 2 + m4
                for kt in range(8):
                    ins = e.matmul(psB[1][:, 2 * m:2 * m + 2], lhsT=wa[b][:, kt, m4 * 128:(m4 + 1) * 128],
                                   rhs=sil[:, kt, :], start=(kt == 0), stop=(kt == 7))
            return ins
        P.add("tensor", mm, reads=[s_wa[b], s_sil], writes=[s_psada])
    ada = sb("ada", [128, 24, 2]); s_ada = S("ada")
    P.add("vector", lambda e: e.tensor_tensor(out=ada[:], in0=psB[1][:, 0:48].rearrange("p (m v) -> p m v", v=2),
                                              in1=bcast(pv[:, PV_BADA:PV_BADA + 24], 2), op=ALU.add),
          reads=[s_psada, s_pv], writes=[s_ada])
    sc1 = sb("sc1", [128, 8, 2]); s_sc1 = S("sc1")
    P.add("vector", lambda e: e.scalar_tensor_tensor(out=sc1[:], in0=ada[:, 8:16, :], scalar=1.0,
                                                     in1=bcast(pv[:, PV_NG:PV_NG + 8], 2), op0=ALU.add, op1=ALU.mult),
          reads=[s_ada, s_pv], writes=[s_sc1])

    h_fm = sb("h_fm", [128, 8, TOK], BF16)
    s_h = [S("h%d" % i) for i in range(TOK // 256)]
    NXB = 3
    xt_all = sb("xt_all", [128, NXB, D]); xt = [xt_all[:, i, :] for i in range(NXB)]; s_xt = [S("xt%d" % i) for i in range(NXB)]
    xn_all = sb("xn_all", [128, 2, D], BF16); xn = [xn_all[:, i, :] for i in range(2)]; s_xn = [S("xn%d" % i) for i in range(2)]
    junk = sb("junk", [128, D], BF16); s_junk = S("junk")
    ssq = [sb("ssq%d" % i, [128, 1]) for i in range(2)]; s_ssq = [S("ssq%d" % i) for i in range(2)]
    rstd = [sb("rstd%d" % i, [128, 1]) for i in range(2)]; s_rstd = [S("rstd%d" % i) for i in range(2)]
    ps_tr = [pst("ps_tr%d" % i, [128, 8, 256], BF16) for i in range(2)]; s_pstr = [Slot("ps_tr%d" % i, True) for i in range(2)]
    tiles = [(ctx_d, i * 128, 1) for i in range(CTX // 128)] + [(x_d, i * 128, 0) for i in range(L // 128)]
    for ti, (src, r0, v) in enumerate(tiles):
        xb = ti % NXB; nb = ti % 2; g = ti // 2; pb = g % 2; half = ti % 2
        P.add("sync", lambda e, src=src, r0=r0, xb=xb: e.dma_start(out=xt[xb][:], in_=src[r0:r0 + 128, :]),
              writes=[s_xt[xb]], dma=True)
        P.add("scalar", lambda e, xb=xb, nb=nb: e.activation(out=junk[:], in_=xt[xb][:], func=AF.Square,
                                                           accum_out=ssq[nb][:]),
              reads=[s_xt[xb]], writes=[s_junk, s_ssq[nb]])
        P.add("scalar", lambda e, nb=nb: e.activation(out=rstd[nb][:], in_=ssq[nb][:], func=AF.Sqrt,
                                                     scale=1.0 / D, bias=EPS),
              reads=[s_ssq[nb]], writes=[s_rstd[nb]])
        P.add("vector", lambda e, nb=nb: e.reciprocal(out=rstd[nb][:], in_=rstd[nb][:]),
              reads=[s_rstd[nb]], writes=[s_rstd[nb]])
        P.add("vector", lambda e, xb=xb, nb=nb: e.tensor_scalar(out=xn[nb], in0=xt[xb][:], scalar1=rstd[nb][:, 0:1],
                                                              scalar2=None, op0=ALU.mult),
              reads=[s_xt[xb], s_rstd[nb]], writes=[s_xn[nb]])

        def tr(e, nb=nb, pb=pb, half=half):
            ins = None
            for kt in range(8):
                ins = e.transpose(out=ps_tr[pb][:, kt, half * 128:(half + 1) * 128],
                                  in_=xn_all[:, nb, kt * 128:(kt + 1) * 128], identity=idb[:])
            return ins
        P.add("tensor", tr, reads=[s_xn[nb], s_idb], writes=[s_pstr[pb]])
        if half == 1:
            for kt in range(8):
                eng = "vector" if kt % 2 == 0 else "scalar"
                if eng == "vector":
                    f = lambda e, kt=kt, pb=pb, g=g, v=v: e.tensor_scalar(
                        out=h_fm[:, kt, g * 256:(g + 1) * 256], in0=ps_tr[pb][:, kt, :],
                        scalar1=sc1[:, kt, v:v + 1], scalar2=ada[:, kt, v:v + 1], op0=ALU.mult, op1=ALU.add)
                else:
                    f = lambda e, kt=kt, pb=pb, g=g, v=v: e.activation(
                        out=h_fm[:, kt, g * 256:(g + 1) * 256], in_=ps_tr[pb][:, kt, :], func=AF.Identity,
                        scale=sc1[:, kt, v:v + 1], bias=ada[:, kt, v:v + 1])
                P.add(eng, f, reads=[s_pstr[pb], s_sc1, s_ada], writes=[s_h[g]])

    stores = []
    if "h_fm" in dbg:
        dd = dram("dbg_h_fm", [128, 8 * TOK], BF16, "ExternalOutput")
        stores.append(P.add("sync", lambda e: e.dma_start(out=dd[:, :], in_=h_fm[:].rearrange("p k t -> p (k t)")),
                            reads=s_h, dma=True))
    if stage <= 1:
        P.add("sync", lambda e: e.nop(), reads=[], writes=[], force=True).deps.update(o.idx for o in stores if o.idx >= 0)
        P.emit(nc)
        es.close()
        return nc

    conv_scr = nc.dram_tensor("conv_scr", [8, 128, L], BF16, kind="Internal").ap()
    HALO = 15 * 64
    big = sb("big", [128, 8, L], BF16)
    s_big = [[S("big%d_%d" % (j, q)) for q in range(NT)] for j in range(8)]
    wst = [sb("wst%d" % i, [128, 8, 128]) for i in range(2)]; s_wst = [S("wst%d" % i) for i in range(2)]
    wb = [sb("wb%d" % i, [128, 8, 128], BF16) for i in range(2)]; s_wb = [S("wb%d" % i) for i in range(2)]
    dg = sb("dg", [128, 32, 128], BF16); s_dg = S("dg")
    sm = sb("sm", [128, 8, 512], BF16)
    cgq = sm[:, 2:6, :].rearrange("p a t -> p (a t)").rearrange("p (s c) -> p s c", c=256)
    s_cgq = [S("sqt0"), S("sqt1"), S("at0"), S("at1")]
    sgt = [sm[:, i, :] for i in range(2)]; s_sgt = [S("sgt%d" % i) for i in range(2)]
    win_v = win_d.rearrange("(kt p) n -> p kt n", p=128)
    wcnt = [0]

    def load_w(dsrc_v, col0, slot_i, dst=None, dslots=None, k0=0, ceng="gpsimd"):
        st = wcnt[0] % 2
        wcnt[0] += 1
        P.add("sync", lambda e: e.dma_start(out=wst[st][:], in_=dsrc_v[:, k0:k0 + 8, col0:col0 + 128]),
              writes=[s_wst[st]], dma=True)
        d_ = wb[slot_i][:] if dst is None else dst
        if ceng == "scalar":
            P.add("scalar", lambda e: e.activation(out=d_, in_=wst[st][:], func=AF.Copy),
                  reads=[s_wst[st]], writes=([s_wb[slot_i]] if dst is None else dslots))
        else:
            P.add(ceng, lambda e: e.tensor_copy(out=d_, in_=wst[st][:]),
                  reads=[s_wst[st]], writes=([s_wb[slot_i]] if dst is None else dslots))

    def proj(ps, s_ps, wslot, col0, ncols, colstep=1, wap=None, wslots=None):
        w_ = wb[wslot] if wap is None else wap
        def f(e):
            ins = None
            for kt in range(8):
                rhs = h_fm[:, kt, col0:col0 + ncols] if colstep == 1 else \
                    rawap(h_fm[:, kt, col0:col0 + 1], [[colstep, ncols]])
                ins = e.matmul(ps[:, 0:ncols], lhsT=w_[:, kt, :], rhs=rhs, start=(kt == 0), stop=(kt == 7))
            return ins
        hs = s_h if colstep != 1 else s_h[col0 // 256:(col0 + ncols - 1) // 256 + 1]
        P.add("tensor", f, reads=([s_wb[wslot]] if wap is None else wslots) + hs, writes=[s_ps])

    R_ = L // 64
    HP = 79
    cg_scr = nc.dram_tensor("cg_scr", [8, 128, L], BF16, kind="Internal").ap()
    s_cgscr = [S("cgs%d" % j) for j in range(8)]
    psT2 = [ps_tr[i][:].rearrange("p k t -> p (k t)").bitcast(F32) for i in range(2)]
    wom = [sb("wom%d" % i, [128, 16, 128], BF16) for i in range(2)]; s_wom = [S("wom%d" % i) for i in range(2)]
    for j in range(8):
        if j == 0 or j == 4:
            P.add("gpsimd", lambda e: e.memset(vpad[:], 0.0), writes=[s_v, s_wa[0], s_wa[1]])
        load_w(win_v, j * 128, 0)
        load_w(win_v, D + j * 128, 1)
        P.add("vector", lambda e, j=j: e.tensor_tensor(
            out=dg[:, 0:31, :], in0=bcast(idf[:], 31, axis=1),
            in1=bcast(pv[:, PV_DW + 31 * j:PV_DW + 31 * j + 31], 128), op=ALU.mult),
            reads=[s_idf, s_pv], writes=[s_dg])
        load_w(win_v, 2 * D + j * 128, 0, dst=wom[0][:, 0:8, :], dslots=[s_wom[0]])
        for nt in range(NT):
            b = nt % 2
            proj(psA[b], s_psA[b], 0, CTX + nt * 512, 512)
            proj(psB[b], s_psB[b], 1, CTX + nt * 512, 512)
            proj(psT2[b], s_pstr[b], 0, CTX + nt * 512, 512, wap=wom[0][:, 0:8, :], wslots=[s_wom[0]])
            c_in = (nt % 4) * 64
            P.add("scalar", lambda e, b=b, c_in=c_in: e.activation(
                out=rawap(cgq[:, 0, c_in:c_in + 1], [[256, 8], [1, 64]]),
                in_=rawap(psT2[b][:, 0:1], [[1, 8], [8, 64]]), func=AF.Silu), reads=[s_pstr[b]], writes=s_cgq)
            if nt % 4 == 3 or nt == NT - 1:
                w_ = c_in + 64
                c_out = (nt // 4) * 256
                P.add("gpsimd", lambda e, j=j, w_=w_, c_out=c_out: e.dma_start(
                    out=cg_scr[j].rearrange("p (s c) -> p s c", c=NCH)[:, :, c_out:c_out + w_], in_=cgq[:, :, 0:w_]),
                    reads=s_cgq, writes=[s_cgscr[j]], dma=True)
            P.add("scalar", lambda e, b=b: e.activation(out=sgt[b][:], in_=psB[b][:], func=AF.Tanh, scale=0.5),
                  reads=[s_psB[b]], writes=[s_sgt[b]])
            if j < 4:
                vo = rawap(vpad[:, nt * 8 * HP + 15:nt * 8 * HP + 16], [[HP, 8], [1, 64]])
                P.add("vector", lambda e, b=b, vo=vo: e.scalar_tensor_tensor(
                    out=vo, in0=sgt[b][:].rearrange("p (r w) -> p r w", w=64), scalar=1.0,
                    in1=psA[b][:].rearrange("p (r w) -> p r w", w=64), op0=ALU.add, op1=ALU.mult),
                    reads=[s_psA[b], s_sgt[b]], writes=[s_v])
            else:
                P.add("vector", lambda e, b=b, nt=nt: e.scalar_tensor_tensor(
                    out=vpad[:, HALO + nt * 512:HALO + (nt + 1) * 512], in0=sgt[b][:], scalar=1.0, in1=psA[b][:],
                    op0=ALU.add, op1=ALU.mult),
                    reads=[s_psA[b], s_sgt[b]], writes=[s_v])
        if j < 4:
            ctiles = [(r0, min(6, R_ - r0)) for r0 in range(0, R_, 6)]
        else:
            ctiles = [(r0, 8) for r0 in range(0, R_, 8)]
        for ci, (r0, nr) in enumerate(ctiles):
            b = ci % 2

            def cv(e, j=j, r0=r0, nr=nr, b=b):
                ins = None
                for tap in range(31):
                    d = tap - 15
                    if j < 4:
                        n = nr * HP
                        base = r0 * HP + 15 + d
                    else:
                        n = nr * 64
                        base = HALO + r0 * 64 + d * 64
                    ins = e.matmul(psA[b][:, 0:n], lhsT=dg[:, tap, :], rhs=vpad[:, base:base + n],
                                   start=(tap == 0), stop=(tap == 30))
                return ins
            P.add("tensor", cv, reads=[s_dg, s_v], writes=[s_psA[b]])
            wrow = HP if j < 4 else 64
            P.add("scalar", lambda e, j=j, r0=r0, nr=nr, b=b, wrow=wrow: e.activation(
                out=rawap(big[:, j, 8 * r0:8 * r0 + 1], [[NCH, 8], [8, nr], [1, 8]]),
                in_=rawap(psA[b][:, 0:1], [[1, 8], [wrow, nr], [8, 8]]), func=AF.Identity, scale=0.5,
                bias=pv[:, PV_CDB + j:PV_CDB + j + 1]),
                reads=[s_psA[b], s_pv], writes=[s_big[j][q] for q in range(NT)])
    if "conv_pre" in dbg:
        dd = dram("dbg_conv_pre", [128, 8 * L], BF16, "ExternalOutput")
        stores.append(P.add("sync", lambda e: e.dma_start(out=dd[:, :], in_=big[:].rearrange("p k t -> p (k t)")),
                            reads=[x for r in s_big for x in r], dma=True))
    if stage <= 2:
        P.add("sync", lambda e: e.nop(), reads=[], writes=[], force=True).deps.update(o.idx for o in stores if o.idx >= 0)
        P.emit(nc)
        es.close()
        return nc

    ones_b = sb("ones_b", [128, 128], BF16); s_ones = S("ones")
    P.add("gpsimd", lambda e: e.memset(ones_b[:], 1.0), writes=[s_ones])
    sqt = [sm[:, 2 + i, :] for i in range(2)]; s_sqt = s_cgq[0:2]
    fzA = sb("fzA", [128, 2])
    fenceA = P.add("gpsimd", lambda e: e.memset(fzA[:], 0.0), writes=s_xt)
    xf32 = xt_all[:].rearrange("p a d -> p (a d)")
    xb16 = xf32.bitcast(BF16)
    f_mean, f_msq, f_var = xf32[:, 0:512], xf32[:, 512:1024], xf32[:, 1024:1536]
    rstdb = [xb16[:, 3072 + 512 * i:3584 + 512 * i] for i in range(2)]
    nmrb = [xb16[:, 4096 + 512 * i:4608 + 512 * i] for i in range(2)]
    f_tb = [xb16[:, 5120 + 512 * i:5632 + 512 * i] for i in range(2)]
    s_2b = [S("ln%d" % i) for i in range(9)]
    for x_ in s_2b:
        x_.last_w = fenceA.idx
    s_mean, s_msq, s_var = s_2b[0:3]
    s_rsb, s_nmb, s_ftb = s_2b[3:5], s_2b[5:7], s_2b[7:9]
    at = [sm[:, 4 + i, :] for i in range(2)]; s_at = s_cgq[2:4]
    cot = [sm[:, 6 + i, :] for i in range(2)]; s_cot = [S("cot%d" % i) for i in range(2)]
    s_scr = [[S("scr%d_%d" % (j, q)) for q in range(NT)] for j in range(8)]

    def pos_cols(q):
        if NCH >= 512:
            s0 = (q * 512) // NCH
            c0 = (q * 512) % NCH
            return CTX + 8 * c0 + s0, [[8, 512]]
        ns = 512 // NCH
        s0 = q * ns
        return CTX + s0, [[1, ns], [8, NCH]]

    def proj_pos(ps, s_ps, wslot, q, wap=None, wslots=None):
        off, dims = pos_cols(q)
        w_ = wb[wslot] if wap is None else wap
        def f(e):
            ins = None
            for kt in range(8):
                ins = e.matmul(ps[:, :], lhsT=w_[:, kt, :], rhs=rawap(h_fm[:, kt, off:off + 1], dims),
                               start=(kt == 0), stop=(kt == 7))
            return ins
        P.add("tensor", f, reads=([s_wb[wslot]] if wap is None else wslots) + s_h, writes=[s_ps])

    cgin = dg[:].rearrange("p t c -> p (t c)").rearrange("p (j t) -> p j t", t=512)

    def ln_stats(q):
        sl = slice(q * 512, (q + 1) * 512); rs = q % 2
        for j in range(8):
            b = j % 2
            P.add("scalar", lambda e, j=j, b=b: e.activation(out=sqt[b][:], in_=big[:, j, sl], func=AF.Square),
                  reads=[s_big[j][q]], writes=[s_sqt[b]])
            P.add("tensor", lambda e, j=j: e.matmul(psA[0][:, :], lhsT=ones_b[:], rhs=big[:, j, sl],
                                                    start=(j == 0), stop=(j == 7)),
                  reads=[s_ones, s_big[j][q]], writes=[s_psA[0]])
            P.add("tensor", lambda e, j=j, b=b: e.matmul(psA[1][:, :], lhsT=ones_b[:], rhs=sqt[b][:],
                                                         start=(j == 0), stop=(j == 7)),
                  reads=[s_ones, s_sqt[b]], writes=[s_psA[1]])
        P.add("scalar", lambda e: e.activation(out=f_mean, in_=psA[0][:], func=AF.Copy, scale=1.0 / D),
              reads=[s_psA[0]], writes=[s_mean])
        P.add("scalar", lambda e: e.activation(out=f_msq, in_=psA[0][:], func=AF.Square, scale=1.0 / D),
              reads=[s_psA[0]], writes=[s_msq])
        P.add("vector", lambda e: e.scalar_tensor_tensor(out=f_var, in0=psA[1][:], scalar=1.0 / D, in1=f_msq,
                                                         op0=ALU.mult, op1=ALU.subtract),
              reads=[s_psA[1], s_msq], writes=[s_var])
        P.add("scalar", lambda e: e.activation(out=f_var, in_=f_var, func=AF.Sqrt, bias=EPS),
              reads=[s_var], writes=[s_var])
        P.add("vector", lambda e: e.reciprocal(out=f_var, in_=f_var), reads=[s_var], writes=[s_var])
        P.add("vector", lambda e: e.tensor_copy(out=rstdb[rs], in_=f_var), reads=[s_var], writes=[s_rsb[rs]])
        P.add("vector", lambda e: e.scalar_tensor_tensor(out=nmrb[rs], in0=f_mean, scalar=-1.0, in1=rstdb[rs],
                                                         op0=ALU.mult, op1=ALU.mult),
              reads=[s_mean, s_rsb[rs]], writes=[s_nmb[rs]])

    def ld_gate(g_):
        q_, j_ = divmod(g_, 8)
        P.add("gpsimd", lambda e: e.dma_start(out=sgt[g_ % 2][:], in_=cg_scr[j_, :, q_ * 512:(q_ + 1) * 512]),
              reads=[s_cgscr[j_]], writes=[s_sgt[g_ % 2]], dma=True)

    def ln_apply(q):
        sl = slice(q * 512, (q + 1) * 512); rs = q % 2

        def pre(j):
            b = j % 2
            P.add("vector", lambda e: e.tensor_tensor(out=f_tb[b], in0=big[:, j, sl], in1=rstdb[rs], op=ALU.mult),
                  reads=[s_big[j][q], s_rsb[rs]], writes=[s_ftb[b]])
            P.add("vector", lambda e: e.tensor_tensor(out=f_tb[b], in0=f_tb[b], in1=nmrb[rs], op=ALU.add),
                  reads=[s_ftb[b], s_nmb[rs]], writes=[s_ftb[b]])
            P.add("scalar", lambda e: e.activation(out=at[b][:], in_=f_tb[b], func=AF.Silu,
                                                   scale=pv[:, PV_LNG + j:PV_LNG + j + 1],
                                                   bias=pv[:, PV_LNB + j:PV_LNB + j + 1]),
                  reads=[s_ftb[b], s_pv], writes=[s_at[b]])

        def post(j):
            b = j % 2
            g_ = q * 8 + j
            P.add("vector", lambda e: e.tensor_tensor(out=cot[b][:], in0=at[b][:], in1=sgt[g_ % 2][:], op=ALU.mult),
                  reads=[s_at[b], s_sgt[g_ % 2]], writes=[s_cot[b]])
            P.add("sync", lambda e: e.dma_start(out=conv_scr[j, :, sl], in_=cot[b][:]),
                  reads=[s_cot[b]], writes=[s_scr[j][q]], dma=True)
            if g_ + 2 < 8 * NT:
                ld_gate(g_ + 2)

        pre(0)
        for j in range(8):
            if j + 1 < 8:
                pre(j + 1)
            post(j)

    ln_stats(0)
    ld_gate(0); ld_gate(1)
    for q in range(NT):
        la = []
        if q + 1 < NT:
            P.capture(); ln_stats(q + 1); la = P.end_capture()
        P.capture(); ln_apply(q); lb = P.end_capture()
        P.add_merged(lb, la)
    P.add("gpsimd", lambda e: e.memset(fzA[:], 0.0), writes=s_2b + s_xt)

    gy_scr = nc.dram_tensor("gy_scr", [8, 128, L], BF16, kind="Internal").ap()
    sg_scr = nc.dram_tensor("sg_scr", [8, 128, L], BF16, kind="Internal").ap()
    s_sgscr = [S("sgs%d" % mt) for mt in range(8)]
    s_gyscr = [[[S("gys%d_%d_%d" % (mt, gl, i)) for i in range(8)] for gl in range(8)] for mt in range(8)]
    if stage >= 4:
        NV = NCH + 2 * CCH
        NF = NCH + CCH
        MAGIC = 12582912.0
        sc_d = dram("ssm_sc", [8, 128, 48])
        bx_d = dram("ssm_bx", [8, 128, 512])
        cx_d = dram("ssm_cx", [8, 128, 512])
        kk_d = dram("kk", [128, 400])
        sgn_d = dram("sgn", [128, 2])
        idx_d = dram("idx2", [128, 2 * NF])
        msk_d = dram("msk", [128, 256])
        dcol_d = dram("dcol", [128, 64])
        fz = sb("fz", [128, 2])
        old_slots = [x for r in s_big for x in r] + s_xt + [s_dg] + s_sgt + s_sqt + s_at + s_cot + [s_v] + s_xn
        fence = P.add("gpsimd", lambda e: e.memset(fz[:], 0.0), writes=old_slots)
        if 16 * L >= 65536:
            arena = big[:].rearrange("p k t -> p (k t)")
        else:
            arena = sb("arena", [128, 32768], BF16)[:]
        aoff = [0]
        arena_slots = []

        arenas = {"main": (arena, 65536), "xt": (xt_all[:].rearrange("p a d -> p (a d)").bitcast(BF16), 12288),
                  "dg": (dg[:].rearrange("p a d -> p (a d)"), 8192), "sm": (sm[:].rearrange("p a d -> p (a d)"), 8192),
                  "vp": (vpad[:], 16384)}
        aoffs = {k: 0 for k in arenas}

        def carve(shape, dtype, name, where="main"):
            n = int(np.prod(shape))
            nb = n * (4 if dtype == F32 else 2)
            ar, cap = arenas[where]
            st = aoffs[where]
            aoffs[where] += (nb + 3) // 4 * 4
            aoff[0] = aoffs["main"]
            assert aoffs[where] <= cap, (where, aoffs[where])
            v = ar[:, st // 2:(st + nb) // 2]
            if dtype == F32:
                v = v.bitcast(F32)
            if len(shape) == 2:
                v = v.rearrange("p (a b) -> p a b", b=shape[1])
            elif len(shape) == 3:
                v = v.rearrange("p (a b c) -> p a b c", b=shape[1], c=shape[2])
            sl = Slot(name)
            sl.last_w = fence.idx
            arena_slots.append(sl)
            return v, sl

        kk, s_kk = carve([16, 25], F32, "kk")
        sgn, s_sgn = carve([2], F32, "sgn")
        idx2, s_idx = carve([2, NF], F32, "idx2")
        msk, s_msk = carve([2, 128], F32, "msk")
        dcol, s_dcol = carve([64], F32, "dcol")
        for dst, src, sl in ((kk, kk_d, s_kk), (sgn, sgn_d, s_sgn), (idx2, idx_d, s_idx), (msk, msk_d, s_msk),
                             (dcol, dcol_d, s_dcol)):
            P.add("sync", lambda e, dst=dst, src=src: e.dma_start(
                out=dst if len(dst.shape) == 2 else dst.rearrange("p a b -> p (a b)"), in_=src[:, :]),
                writes=[sl], dma=True)
        sc_t, s_sc = carve([48], F32, "sc_t")
        bx_t, s_bx = carve([2, 16, 16], F32, "bx_t")
        cx_t, s_cx = carve([2, 16, 16], F32, "cx_t")
        NPM = 24
        pm, _ = carve([NPM, 16], F32, "pm")
        s_pm = [Slot("pm%d" % i) for i in range(NPM)]
        for x in s_pm:
            x.last_w = fence.idx
        arena_slots.extend(s_pm)
        (R_DT, R_MAG, R_ANG, R_A, R_P1, R_P2, R_NR, R_NI, R_DEN, R_CR, R_CI, R_CIA, R_CIB, R_RHO, R_F8, R_U, R_K,
         R_FCH) = range(18)
        NTB = 7
        tbl, _ = carve([NTB, 16, 25], F32, "tbl")
        s_tb = [Slot("tbl%d" % i) for i in range(NTB)]
        for x in s_tb:
            x.last_w = fence.idx
        arena_slots.extend(s_tb)
        (T_T, T_U, T_EM, T_SIN, T_COS, T_LR, T_LI) = range(7)
        T_LRX, T_LIS, T_NEG = T_SIN, T_COS, T_EM
        bb, s_bb = carve([2, 16, 16], F32, "bb")
        gtmp, s_gtmp = carve([2, 8, 16], F32, "gtmp")
        Uq, s_uq = carve([8, NV], BF16, "Uq")
        Vt = []; s_vt = []
        for gl in range(8):
            a, b_ = carve([NV], BF16, "V%d" % gl)
            sl8 = [b_] + [Slot("V%d_%d" % (gl, i)) for i in range(1, 8)]
            for x in sl8:
                x.last_w = fence.idx
            arena_slots.extend(sl8[1:])
            Vt.append(a); s_vt.append(sl8)
        Yq = []; s_yq = []
        for i in range(2):
            a, b_ = carve([NCH], BF16, "Yq%d" % i)
            Yq.append(a); s_yq.append(b_)
        gen = {}
        for nm in ("Ba0", "Bb0", "MT0", "Ba1", "Bb1", "MT1", "dgm"):
            gen[nm] = carve([8, 16], BF16, nm)
        GEN = {}
        for nm, wh in (("BaT", "xt"), ("BbT", "xt"), ("Q", "xt"), ("CT", "dg"), ("Cb", "dg")):
            GEN[nm] = carve([16, 128], BF16, "G" + nm, wh)
        gA, s_gA = carve([8, 8, 16], BF16, "gA", "sm")
        gB, s_gB = carve([8, 8, 16], BF16, "gB", "sm")
        tabs = {}
        for nm in ("t", "u", "fr", "SIN", "COS", "t1"):
            tabs[nm] = carve([NF], F32, nm)
        tabs["afr"] = tabs["u"]
        tabs["t2"] = tabs["t"]
        tabs["G"] = tabs["u"]
        Gc, s_gc = carve([NF], BF16, "Gc")
        Gs, s_gs = carve([NF], BF16, "Gs")
        ghy = sb("ghy", [128, 512], BF16); s_ghy = S("ghy"); s_gz = S("gz")
        sgq = xn_all[:].rearrange("p a d -> p (a d)").rearrange("p (s c) -> p s c", c=256)
        s_sgq = Slot("sgq"); s_sgq.last_w = fence.idx; arena_slots.append(s_sgq)
        TS = [dict(u=tabs["u"], fr=tabs["fr"], SIN=tabs["SIN"], COS=tabs["COS"], t1=tabs["t1"], t2=tabs["t"],
                   Gc=(Gc, s_gc), Gs=(Gs, s_gs))]
        t1b = {}
        for nm in ("t", "u", "fr", "SIN", "COS", "t1"):
            t1b[nm] = carve([NF], F32, nm + "_b", "vp")
        Gc1 = carve([NF], BF16, "Gc_b", "vp"); Gs1 = carve([NF], BF16, "Gs_b", "vp")
        TS.append(dict(u=t1b["u"], fr=t1b["fr"], SIN=t1b["SIN"], COS=t1b["COS"], t1=t1b["t1"], t2=t1b["t"], Gc=Gc1, Gs=Gs1))
        print("ssm arena bytes", aoff[0])

        PG = lambda f, r, w: P.add("gpsimd", f, reads=r, writes=w)
        PV_ = lambda f, r, w: P.add("vector", f, reads=r, writes=w)
        PA = lambda f, r, w: P.add("scalar", f, reads=r, writes=w)
        pmr = lambda i: pm[:, i, :]
        ZaP = ps_tr[0][:].rearrange("p k t -> p (k t)").bitcast(F32)
        ZbP = ps_tr[1][:].rearrange("p k t -> p (k t)").bitcast(F32)
        psBb = [psB[i][:].bitcast(BF16) for i in range(2)]
        pieces = [(0, min(512, NF))] + ([(512, NF)] if NF > 512 else [])

        def rev(ap2, n):
            d0 = list(ap2.ap[0]); d1 = list(ap2.ap[1])
            return bass.AP(ap2.tensor, ap2.offset + (n - 1) * d1[0], [d0, [-d1[0], n]])

        for mt in range(8):
            def load_params(m_):
                P.add("sync", lambda e: e.dma_start(out=sc_t, in_=sc_d[m_, :, :]), writes=[s_sc], dma=True)
                P.add("sync", lambda e: e.dma_start(out=bx_t.rearrange("p a b c -> p (a b c)"), in_=bx_d[m_, :, :]),
                      writes=[s_bx], dma=True)
                P.add("sync", lambda e: e.dma_start(out=cx_t.rearrange("p a b c -> p (a b c)"), in_=cx_d[m_, :, :]),
                      writes=[s_cx], dma=True)
            if mt == 0:
                load_params(0)
            are, aim, ldt = sc_t[:, 0:16], sc_t[:, 16:32], sc_t[:, 32:48]
            PA(lambda e: e.activation(out=pmr(R_DT), in_=ldt, func=AF.Exp), [s_sc], [s_pm[R_DT]])
            PV_(lambda e: e.tensor_tensor(out=pmr(R_MAG), in0=are, in1=pmr(R_DT), op=ALU.mult), [s_sc, s_pm[R_DT]], [s_pm[R_MAG]])
            PV_(lambda e: e.scalar_tensor_tensor(out=pmr(R_ANG), in0=aim, scalar=1.0 / TWO_PI, in1=pmr(R_DT),
                                                 op0=ALU.mult, op1=ALU.mult), [s_sc, s_pm[R_DT]], [s_pm[R_ANG]])
            T = lambda i: tbl[:, i, :, :]
            PV_(lambda e: e.tensor_tensor(out=T(T_EM), in0=bcast(pmr(R_MAG), 25), in1=kk, op=ALU.mult),
               [s_pm[R_MAG], s_kk], [s_tb[T_EM]])
            PA(lambda e: e.activation(out=T(T_EM), in_=T(T_EM), func=AF.Exp), [s_tb[T_EM]], [s_tb[T_EM]])
            PV_(lambda e: e.tensor_tensor(out=T(T_T), in0=bcast(pmr(R_ANG), 25), in1=kk, op=ALU.mult),
               [s_pm[R_ANG], s_kk], [s_tb[T_T]])
            PA(lambda e: e.activation(out=T(T_U).bitcast(I32), in_=T(T_T), func=AF.Identity), [s_tb[T_T]], [s_tb[T_U]])
            PV_(lambda e: e.tensor_tensor(out=T(T_T), in0=T(T_T), in1=T(T_U).bitcast(I32), op=ALU.subtract),
               [s_tb[T_T], s_tb[T_U]], [s_tb[T_T]])
            PA(lambda e: e.activation(out=T(T_U), in_=T(T_T), func=AF.Abs), [s_tb[T_T]], [s_tb[T_U]])
            PA(lambda e: e.activation(out=T(T_SIN), in_=T(T_T), func=AF.Sin, scale=TWO_PI), [s_tb[T_T]], [s_tb[T_SIN]])
            PA(lambda e: e.activation(out=T(T_COS), in_=T(T_U), func=AF.Sin, scale=-TWO_PI, bias=math.pi / 2),
               [s_tb[T_U]], [s_tb[T_COS]])
            PV_(lambda e: e.tensor_tensor(out=T(T_LR), in0=T(T_EM), in1=T(T_COS), op=ALU.mult),
               [s_tb[T_EM], s_tb[T_COS]], [s_tb[T_LR]])
            PV_(lambda e: e.tensor_tensor(out=T(T_LI), in0=T(T_EM), in1=T(T_SIN), op=ALU.mult),
               [s_tb[T_EM], s_tb[T_SIN]], [s_tb[T_LI]])
            PV_(lambda e: e.tensor_scalar(out=T(T_LRX), in0=T(T_LR), scalar1=sgn[:, 0:1], scalar2=None, op0=ALU.mult),
               [s_tb[T_LR], s_sgn], [s_tb[T_LRX]])
            PV_(lambda e: e.tensor_scalar(out=T(T_LIS), in0=T(T_LI), scalar1=sgn[:, 1:2], scalar2=None, op0=ALU.mult),
               [s_tb[T_LI], s_sgn], [s_tb[T_LIS]])
            PV_(lambda e: e.tensor_scalar(out=T(T_NEG), in0=T(T_LR), scalar1=-1.0, scalar2=None, op0=ALU.mult),
               [s_tb[T_LR]], [s_tb[T_NEG]])
            PV_(lambda e: e.tensor_scalar(out=T(T_U), in0=T(T_LI), scalar1=-1.0, scalar2=None, op0=ALU.mult),
               [s_tb[T_LI]], [s_tb[T_U]])
            LRd, LRx, LRn, LIs, LIy, LIn = T_LR, T_LRX, T_NEG, T_LIS, T_LI, T_U
            lbr, lbi = tbl[:, T_LR, :, 24], tbl[:, T_LI, :, 24]
            TT = lambda o, a, b_, op, rd, wr: PV_(lambda e: e.tensor_tensor(out=o, in0=a, in1=b_, op=op), rd, wr)
            PV_(lambda e: e.tensor_scalar(out=pmr(R_A), in0=lbr, scalar1=-1.0, scalar2=None, op0=ALU.add),
               [s_tb[T_LR]], [s_pm[R_A]])
            TT(pmr(R_P1), pmr(R_A), are, ALU.mult, [s_pm[R_A], s_sc], [s_pm[R_P1]])
            TT(pmr(R_P2), lbi, aim, ALU.mult, [s_tb[T_LI], s_sc], [s_pm[R_P2]])
            TT(pmr(R_NR), pmr(R_P1), pmr(R_P2), ALU.add, [s_pm[R_P1], s_pm[R_P2]], [s_pm[R_NR]])
            TT(pmr(R_P1), lbi, are, ALU.mult, [s_tb[T_LI], s_sc, s_pm[R_NR]], [s_pm[R_P1]])
            TT(pmr(R_P2), pmr(R_A), aim, ALU.mult, [s_pm[R_A], s_sc, s_pm[R_NR]], [s_pm[R_P2]])
            TT(pmr(R_NI), pmr(R_P1), pmr(R_P2), ALU.subtract, [s_pm[R_P1], s_pm[R_P2]], [s_pm[R_NI]])
            TT(pmr(R_P1), are, are, ALU.mult, [s_sc, s_pm[R_NI]], [s_pm[R_P1]])
            TT(pmr(R_P2), aim, aim, ALU.mult, [s_sc, s_pm[R_NI]], [s_pm[R_P2]])
            TT(pmr(R_DEN), pmr(R_P1), pmr(R_P2), ALU.add, [s_pm[R_P1], s_pm[R_P2]], [s_pm[R_DEN]])
            PV_(lambda e: e.reciprocal(out=pmr(R_DEN), in_=pmr(R_DEN)), [s_pm[R_DEN]], [s_pm[R_DEN]])
            TT(pmr(R_CR), pmr(R_NR), pmr(R_DEN), ALU.mult, [s_pm[R_NR], s_pm[R_DEN]], [s_pm[R_CR]])
            TT(pmr(R_CI), pmr(R_NI), pmr(R_DEN), ALU.mult, [s_pm[R_NI], s_pm[R_DEN]], [s_pm[R_CI]])
            PV_(lambda e: e.tensor_scalar(out=pmr(R_CIA), in0=pmr(R_CI), scalar1=sgn[:, 1:2], scalar2=None, op0=ALU.mult),
               [s_pm[R_CI], s_sgn], [s_pm[R_CIA]])
            PV_(lambda e: e.tensor_scalar(out=pmr(R_CIB), in0=pmr(R_CI), scalar1=sgn[:, 0:1], scalar2=None, op0=ALU.mult),
               [s_pm[R_CI], s_sgn], [s_pm[R_CIB]])
            X1, X2 = bx_t[:, 0, :, :], bx_t[:, 1, :, :]
            g0 = gtmp.rearrange("p a b c -> p (a b c)")[:, 0:256].rearrange("p (a b) -> p a b", b=16)
            TT(bb[:, 0, :, :], bcast(pmr(R_CR), 16), X1, ALU.mult, [s_pm[R_CR], s_bx], [s_bb])
            TT(g0, bcast(pmr(R_CIA), 16), X2, ALU.mult, [s_pm[R_CIA], s_bx], [s_gtmp])
            TT(bb[:, 0, :, :], bb[:, 0, :, :], g0, ALU.add, [s_bb, s_gtmp], [s_bb])
            TT(bb[:, 1, :, :], bcast(pmr(R_CR), 16), X2, ALU.mult, [s_pm[R_CR], s_bx, s_bb], [s_bb])
            TT(g0, bcast(pmr(R_CIB), 16), X1, ALU.mult, [s_pm[R_CIB], s_bx, s_bb], [s_gtmp])
            TT(bb[:, 1, :, :], bb[:, 1, :, :], g0, ALU.add, [s_bb, s_gtmp], [s_bb])
            PA(lambda e: e.activation(out=pmr(R_RHO), in_=pmr(R_MAG), func=AF.Exp, scale=8.0), [s_pm[R_MAG]], [s_pm[R_RHO]])
            PV_(lambda e: e.tensor_scalar(out=pmr(R_F8), in0=pmr(R_ANG), scalar1=8.0, scalar2=None, op0=ALU.mult),
               [s_pm[R_ANG]], [s_pm[R_F8]])
            PA(lambda e: e.activation(out=pmr(R_U).bitcast(I32), in_=pmr(R_F8), func=AF.Identity), [s_pm[R_F8]], [s_pm[R_U]])
            TT(pmr(R_FCH), pmr(R_F8), pmr(R_U).bitcast(I32), ALU.subtract, [s_pm[R_F8], s_pm[R_U]], [s_pm[R_FCH]])

            def realform_b(name, La, Wi, Lb, Wj, koff, Wt, s_w):
                o, so = GEN[name]
                for hf in range(2):
                    es_ = slice(8 * hf, 8 * hf + 8)
                    la = bcast(tbl[:, La, es_, koff:koff + 8], 16)
                    lb = bcast(tbl[:, Lb, es_, koff:koff + 8], 16)
                    wa_ = bcast(Wt[:, Wi, es_, :], 8, axis=2)
                    wb_ = bcast(Wt[:, Wj, es_, :], 8, axis=2)
                    oo = o[:, es_, :].rearrange("p e (s h) -> p e s h", h=16)
                    PV_(lambda e, la=la, wa_=wa_: e.tensor_tensor(out=gA, in0=la, in1=wa_, op=ALU.mult),
                        [s_tb[La], s_w], [s_gA])
                    PV_(lambda e, lb=lb, wb_=wb_: e.tensor_tensor(out=gB, in0=lb, in1=wb_, op=ALU.mult),
                        [s_tb[Lb], s_w], [s_gB])
                    PV_(lambda e, oo=oo: e.tensor_tensor(out=oo, in0=gA, in1=gB, op=ALU.add), [s_gA, s_gB], [so])
            realform_b("BaT", LRd, 0, LIs, 1, 0, bb, s_bb)
            realform_b("BbT", LRx, 1, LIy, 0, 0, bb, s_bb)
            realform_b("Q", LRd, 0, LIs, 1, 8, bb, s_bb)
            realform_b("CT", LRx, 0, LIn, 1, 16, cx_t, s_cx)
            realform_b("Cb", LRn, 1, LIs, 0, 16, cx_t, s_cx)
            if mt == 0:
                load_w(win_v, 3 * D + mt * 128, 0)
            proj(psA[0], s_psA[0], 0, 0, CTX)
            for c0 in (0, CCH + NCH):
                PA(lambda e, c0=c0: e.activation(out=rawap(Uq[:, 0, c0:c0 + 1], [[NV, 8], [1, CCH]]),
                                                 in_=rawap(psA[0][:, 0:1], [[1, 8], [8, CCH]]), func=AF.Copy),
                   [s_psA[0]], [s_uq])
            for nt in range(NT):
                b = (nt + 1) % 2
                proj(psA[b], s_psA[b], 0, CTX + nt * 512, 512)
                PA(lambda e, nt=nt, b=b: e.activation(
                    out=rawap(Uq[:, 0, CCH + 64 * nt:CCH + 64 * nt + 1], [[NV, 8], [1, 64]]),
                    in_=rawap(psA[b][:, 0:1], [[1, 8], [8, 64]]), func=AF.Copy), [s_psA[b]], [s_uq])
            if mt == 0:
                load_w(win_v, 4 * D + mt * 128, 1)
            TPH = 4
            for nt in range(NT):
                b = nt % 2
                proj(psA[b], s_psA[b], 1, CTX + nt * 512, 512)
                c_in = (nt % TPH) * 64
                PA(lambda e, b=b, c_in=c_in: e.activation(
                    out=rawap(sgq[:, 0, c_in:c_in + 1], [[256, 8], [1, 64]]),
                    in_=rawap(psA[b][:, 0:1], [[1, 8], [8, 64]]), func=AF.Silu), [s_psA[b]], [s_sgq])
                if nt % TPH == TPH - 1 or nt == NT - 1:
                    w_ = c_in + 64
                    c_out = (nt // TPH) * 256
                    P.add("gpsimd", lambda e, mt=mt, w_=w_, c_out=c_out: e.dma_start(
                        out=sg_scr[mt].rearrange("p (s c) -> p s c", c=NCH)[:, :, c_out:c_out + w_], in_=sgq[:, :, 0:w_]),
                        reads=[s_sgq], writes=[s_sgscr[mt]], dma=True)
            for gl in range(8):
                for s_ in range(8):
                    P.add("sync" if s_ % 2 == 0 else "gpsimd", lambda e, gl=gl, s_=s_: e.dma_start(
                        out=Vt[gl][16 * s_:16 * s_ + 16, :], in_=Uq[16 * gl:16 * gl + 16, s_, :]),
                        reads=[s_uq], writes=[s_vt[gl][s_]], dma=True)
            if mt + 1 < 8:
                load_params(mt + 1)
                load_w(win_v, 3 * D + (mt + 1) * 128, 0)
                load_w(win_v, 4 * D + (mt + 1) * 128, 1)

            f2 = lambda nm: gen[nm][0].rearrange("p a b -> p (a b)")
            sG = lambda nm: GEN[nm][1]
            items = [(gl, d) for gl in range(8) for d in range(2)]

            h16 = lambda ap: ap.bitcast(BF16)[:, 0:NF]

            def stage_T(k, part=0):
                gl, d = items[k]; e_ = d * 8 + gl; T_ = TS[k % 2]
                tu, s_u = T_["u"]; tfr, s_fr = T_["fr"]; SINt, s_sin = T_["SIN"]; COSt, s_cos = T_["COS"]
                SINt = h16(SINt); COSt = h16(COSt)
                fcol = pm[:, R_FCH, e_:e_ + 1]
                tui = tu.bitcast(I32)
                if part in (0, 1):
                    PA(lambda e: e.activation(out=tui, in_=idx2[:, d, :], func=AF.Identity, scale=fcol),
                       [s_idx, s_pm[R_FCH]], [s_u])
                if part == 1:
                    return
                PV_(lambda e: e.scalar_tensor_tensor(out=tfr, in0=idx2[:, d, :], scalar=fcol, in1=tui,
                                                     op0=ALU.mult, op1=ALU.subtract), [s_idx, s_pm[R_FCH], s_u], [s_fr])
                PA(lambda e: e.activation(out=tu, in_=tfr, func=AF.Abs), [s_fr], [s_u])
                PA(lambda e: e.activation(out=SINt, in_=tfr, func=AF.Sin, scale=TWO_PI), [s_fr], [s_sin])
                PA(lambda e: e.activation(out=COSt, in_=tu, func=AF.Sin, scale=-TWO_PI, bias=math.pi / 2), [s_u], [s_cos])

            def stage_G(k):
                gl, d = items[k]; e_ = d * 8 + gl
                Gm = lambda nm: GEN[nm][0][:, e_, :]
                Ban, Bbn, MTn = "Ba%d" % d, "Bb%d" % d, "MT%d" % d
                P.add("tensor", lambda e: e.transpose(out=psBb[d][:, 0:128], in_=Gm("BaT"), identity=idb[:]),
                      reads=[sG("BaT"), s_idb], writes=[s_psB[d]])
                P.add("tensor", lambda e: e.transpose(out=psBb[d][:, 128:256], in_=Gm("BbT"), identity=idb[:]),
                      reads=[sG("BbT"), s_idb], writes=[s_psB[d]])
                P.add("tensor", lambda e: e.matmul(psB[d][:, 256:384], lhsT=Gm("Q"), rhs=Gm("CT"), start=True, stop=True),
                      reads=[sG("Q"), sG("CT")], writes=[s_psB[d]])
                PA(lambda e: e.activation(out=rawap(f2(Ban), [[1, 128]]) if False else f2(Ban), in_=psBb[d][:, 0:128],
                                          func=AF.Copy), [s_psB[d]], [gen[Ban][1]])
                PA(lambda e: e.activation(out=f2(Bbn), in_=psBb[d][:, 128:256], func=AF.Copy), [s_psB[d]], [gen[Bbn][1]])
                PV_(lambda e: e.tensor_tensor(out=f2(MTn), in0=psB[d][:, 256:384], in1=msk[:, d, :], op=ALU.mult),
                    [s_psB[d], s_msk], [gen[MTn][1]])

            def stage_B(k):
                gl, d = items[k]; V = Vt[gl]
                Ban, Bbn = "Ba%d" % d, "Bb%d" % d
                col0 = 0 if d == 0 else CCH
                for (a0, a1) in pieces:
                    P.add("tensor", lambda e, a0=a0, a1=a1: e.matmul(
                        ZaP[:, a0:a1], lhsT=f2(Ban), rhs=V[:, col0 + a0:col0 + a1], start=True, stop=True),
                        reads=[gen[Ban][1]] + s_vt[gl], writes=[s_pstr[0]])
                    P.add("tensor", lambda e, a0=a0, a1=a1: e.matmul(
                        ZbP[:, a0:a1], lhsT=f2(Bbn), rhs=V[:, col0 + a0:col0 + a1], start=True, stop=True),
                        reads=[gen[Bbn][1]] + s_vt[gl], writes=[s_pstr[1]])

            def stage_M(k):
                T_ = TS[k % 2]
                SINt, s_sin = T_["SIN"]; COSt, s_cos = T_["COS"]; t1, s_t1 = T_["t1"]; t2, s_t2 = T_["t2"]
                SINt = h16(SINt); COSt = h16(COSt); t1 = h16(t1); t2 = h16(t2)
                for (a0, a1) in pieces:
                    PV_(lambda e, a0=a0, a1=a1: e.tensor_tensor(out=t1[:, a0:a1], in0=ZaP[:, a0:a1], in1=COSt[:, a0:a1],
                                                               op=ALU.mult), [s_pstr[0], s_cos], [s_t1])
                    PV_(lambda e, a0=a0, a1=a1: e.tensor_tensor(out=t2[:, a0:a1], in0=ZbP[:, a0:a1], in1=SINt[:, a0:a1],
                                                               op=ALU.mult), [s_pstr[1], s_sin], [s_t2])
                PV_(lambda e: e.tensor_tensor(out=t1, in0=t1, in1=t2, op=ALU.add), [s_t1, s_t2], [s_t1])

            def stage_S(k):
                gl, d = items[k]; e_ = d * 8 + gl; T_ = TS[k % 2]
                SINt, s_sin = T_["SIN"]; COSt, s_cos = T_["COS"]; t1, s_t1 = T_["t1"]; G, s_G = T_["u"]
                SINt = h16(SINt); COSt = h16(COSt); t1 = h16(t1); G = h16(G)
                Gc_, s_gc_ = T_["Gc"]; Gs_, s_gs_ = T_["Gs"]
                rho_b = bass.AP(pm.tensor, pm[:, R_RHO, e_:e_ + 1].offset, [list(pm.ap[0]), [0, NF]])
                if d == 0:
                    PV_(lambda e: e.tensor_tensor_scan(out=G, data0=rho_b, data1=t1, initial=0.0, op0=ALU.mult, op1=ALU.add),
                        [s_t1, s_pm[R_RHO]], [s_G])
                else:
                    PV_(lambda e: e.tensor_tensor_scan(out=rev(G, NF), data0=rho_b, data1=rev(t1, NF), initial=0.0,
                                                       op0=ALU.mult, op1=ALU.add), [s_t1, s_pm[R_RHO]], [s_G])
                PV_(lambda e: e.tensor_tensor(out=Gc_, in0=G, in1=COSt, op=ALU.mult), [s_G, s_cos], [s_gc_])
                PV_(lambda e: e.tensor_tensor(out=Gs_, in0=G, in1=SINt, op=ALU.mult), [s_G, s_sin], [s_gs_])

            def stage_Y(k):
                gl, d = items[k]; e_ = d * 8 + gl; T_ = TS[k % 2]
                g = 8 * mt + gl; yb = gl % 2; V = Vt[gl]
                Gm = lambda nm: GEN[nm][0][:, e_, :]
                MTn = "MT%d" % d
                Gc_, s_gc_ = T_["Gc"]; Gs_, s_gs_ = T_["Gs"]
                if d == 0:
                    dgm, s_dgm = gen["dgm"]
                    PV_(lambda e: e.tensor_scalar(out=dgm.rearrange("p a b -> p (a b)"), in0=idf[:],
                                                  scalar1=dcol[:, g:g + 1], scalar2=None, op0=ALU.mult),
                        [s_idf, s_dcol], [s_dgm])
                    P.add("tensor", lambda e: e.matmul(psA[yb][:, 0:NCH], lhsT=dgm.rearrange("p a b -> p (a b)"),
                                                       rhs=V[:, CCH:CCH + NCH], start=True, stop=False),
                          reads=[s_dgm] + s_vt[gl], writes=[s_psA[yb]])
                pc = CCH - 1 if d == 0 else 1
                P.add("tensor", lambda e: e.matmul(psA[yb][:, 0:NCH], lhsT=f2(MTn), rhs=V[:, CCH:CCH + NCH],
                                                   start=False, stop=False),
                      reads=[gen[MTn][1]] + s_vt[gl], writes=[s_psA[yb]])
                P.add("tensor", lambda e: e.matmul(psA[yb][:, 0:NCH], lhsT=Gm("CT"), rhs=Gc_[:, pc:pc + NCH],
                                                   start=False, stop=False),
                      reads=[sG("CT"), s_gc_], writes=[s_psA[yb]])
                P.add("tensor", lambda e: e.matmul(psA[yb][:, 0:NCH], lhsT=Gm("Cb"), rhs=Gs_[:, pc:pc + NCH],
                                                   start=False, stop=(d == 1)),
                      reads=[sG("Cb"), s_gs_], writes=[s_psA[yb]])

            def stage_E(k):
                gl, d = items[k]; T_ = TS[k % 2]
                yb = gl % 2
                Gc_, s_gc_ = T_["Gc"]
                if d == 1:
                    y2, s_y2 = junk[:, 0:NCH], s_junk
                    z_, s_z = junk[:, 512:512 + NCH], s_gz
                    Yp = psA[yb][:, 0:NCH]
                    PA(lambda e: e.activation(out=y2, in_=Yp, func=AF.Square), [s_psA[yb]], [s_y2])
                    PV_(lambda e: e.tensor_scalar(out=y2, in0=y2, scalar1=0.044715, scalar2=1.0, op0=ALU.mult, op1=ALU.add),
                        [s_y2], [s_y2])
                    PV_(lambda e: e.tensor_tensor(out=z_, in0=Yp, in1=y2, op=ALU.mult), [s_psA[yb], s_y2], [s_z])
                    hy, s_hy = ghy[:, 0:NCH], s_ghy
                    PA(lambda e: e.activation(out=z_, in_=z_, func=AF.Tanh, scale=0.7978845608028654), [s_z], [s_z])
                    PA(lambda e: e.activation(out=hy, in_=Yp, func=AF.Copy, scale=0.5), [s_psA[yb]], [s_hy])
                    PV_(lambda e: e.scalar_tensor_tensor(out=Yq[yb], in0=z_, scalar=1.0, in1=hy, op0=ALU.add, op1=ALU.mult),
                        [s_z, s_hy], [s_yq[yb]])
                    for s_ in range(8):
                        P.add("gpsimd" if s_ % 2 == 0 else "sync", lambda e, s_=s_, mt=mt: e.dma_start(
                            out=gy_scr[mt, 16 * gl:16 * gl + 16, s_ * NCH:(s_ + 1) * NCH], in_=Yq[yb][16 * s_:16 * s_ + 16, :]),
                            reads=[s_yq[yb]], writes=[s_gyscr[mt][gl][s_]], dma=True)

            stage_T(0); stage_G(0); stage_B(0)
            lc = []
            for k in range(16):
                la = []
                if k + 1 < 16:
                    P.capture(); stage_T(k + 1, 1); stage_G(k + 1); stage_T(k + 1, 2); la = P.end_capture()
                P.capture(); stage_M(k); lb = P.end_capture()
                P.add_merged(la, lb, lc)
                if k + 1 < 16:
                    stage_B(k + 1)
                stage_S(k)
                stage_Y(k)
                P.capture(); stage_E(k); lc = P.end_capture()
            P.add_merged(lc)
        if "gy" in dbg:
            dd = dram("dbg_gy", [8, 128, L], BF16, "ExternalOutput")
            gyt, s_gyt = carve([8, L], BF16, "gyt") if 16 * L < 65536 else (None, None)
            for mt in range(8):
                P.add("sync", lambda e, mt=mt: e.dma_start(out=gyt[:, mt, :], in_=gy_scr[mt, :, :]),
                      reads=[y_ for x in s_gyscr[mt] for y_ in x], writes=[s_gyt], dma=True)
            stores.append(P.add("sync", lambda e: e.dma_start(out=dd.rearrange("m p t -> p m t"), in_=gyt),
                                reads=[s_gyt], dma=True))
        if stage == 4:
            P.add("sync", lambda e: e.nop(), reads=[], writes=[], force=True).deps.update(o.idx for o in stores if o.idx >= 0)
            P.emit(nc)
            es.close()
            return nc

    wout_d = dram("w_out", [2 * D, D])
    fg_d = dram("fg_b", [128, D])
    fg = xn_all[:].rearrange("p a d -> p (a d)").bitcast(F32); s_fg = s_xn[0]
    P.add("sync", lambda e: e.dma_start(out=fg, in_=fg_d[:, :]), writes=[s_xn[0], s_xn[1]], dma=True)
    wout_v = wout_d.rearrange("(kt p) n -> p kt n", p=128)

    def load_wo(m, slot):
        for half in range(2):
            st = wcnt[0] % 2
            wcnt[0] += 1
            P.add("sync", lambda e, st=st, half=half: e.dma_start(
                out=wst[st][:], in_=wout_v[:, half * 8:(half + 1) * 8, m * 128:(m + 1) * 128]),
                writes=[s_wst[st]], dma=True)
            P.add("gpsimd", lambda e, st=st, half=half: e.tensor_copy(
                out=wom[slot][:, half * 8:(half + 1) * 8, :], in_=wst[st][:]),
                reads=[s_wst[st]], writes=[s_wom[slot]])
    cin1 = dg[:].rearrange("p t c -> p (t c)").rearrange("p (j t) -> p j t", t=512)
    mixg = vpad[:].bitcast(F32).rearrange("p (m t) -> p m t", t=512); s_mixg = s_v
    xres = [xt[0], xt[1]]; s_xres = [s_xt[0], s_xt[1]]
    xo = [xt[2], sm[:, 0:4, :].rearrange("p a t -> p (a t)").bitcast(F32)]
    s_xo = [[s_xt[2]], [s_sgt[0], s_sgt[1], s_sqt[0], s_sqt[1]]]
    psT = [ps_tr[i][:].rearrange("p k t -> p (k t)").bitcast(F32) for i in range(2)]
    x_sc = x_d.rearrange("(c s) d -> s c d", s=8)
    y_sc = y_d.rearrange("(c s) d -> s c d", s=8)
    K_TILES = 8 if stage < 5 else 16
    if stage >= 5:
        glu_d = dram("glu_w", [D, D])
        glu_v = glu_d.rearrange("(kt p) n -> p kt n", p=128)
        fence2 = P.add("gpsimd", lambda e: e.memset(fz[:], 0.0), writes=arena_slots + old_slots)
        aoffs["main"] = 0
        aoff[0] = 0
        _c = carve
        gin, s_gin = _c([8, 512], BF16, "gin"); s_gin.last_w = fence2.idx
        sso, s_sso = _c([8, 512], BF16, "sso"); s_sso.last_w = fence2.idx
        sgx = sb("sgx", [128, 2, 512], BF16)
        sg2, s_sg2 = sgx[:, 0, :], S("sg2")
        sgate, s_sgate = sgx[:, 1, :], S("sgate")
        wo_r, s_wor = _c([8, 16, 128], BF16, "wo_r"); s_wor.last_w = fence2.idx
        wglu_r, s_wgr = _c([8, 8, 128], BF16, "wglu_r"); s_wgr.last_w = fence2.idx
        s_wor_l = [s_wor] + [Slot("wor%d" % i) for i in range(1, 8)]
        s_wgr_l = [s_wgr] + [Slot("wgr%d" % i) for i in range(1, 8)]
        for x_ in s_wor_l[1:] + s_wgr_l[1:]:
            x_.last_w = fence2.idx
        for m in range(8):
            for half in range(2):
                load_w(wout_v, m * 128, 0, dst=wo_r[:, m, half * 8:(half + 1) * 8, :], dslots=[s_wor_l[m]], k0=half * 8,
                       ceng=("vector", "scalar")[half])
        for m2 in range(8):
            load_w(glu_v, m2 * 128, 0, dst=wglu_r[:, m2, :, :], dslots=[s_wgr_l[m2]], ceng=("vector", "scalar")[m2 % 2])
    def load_gin(q):
        sl = slice(q * 512, (q + 1) * 512)
        P.add("gpsimd", lambda e, sl=sl: e.dma_start(out=gin, in_=gy_scr[:, :, sl].rearrange("m p t -> p m t")),
              reads=[y_ for r in s_gyscr for x in r for y_ in x], writes=[s_gin], dma=True)
        for hf in range(2):
            P.add("gpsimd", lambda e, sl=sl, hf=hf: e.dma_start(
                out=wom[hf][:, 0:4, :].rearrange("p m (a t) -> p (m a) t", t=512) if False else sgin[hf],
                in_=sg_scr[4 * hf:4 * hf + 4, :, sl].rearrange("m p t -> p m t")),
                reads=s_sgscr, writes=[s_wom[hf]], dma=True)
    if stage >= 5:
        sgin = [wom[hf][:].rearrange("p k c -> p (k c)").rearrange("p (m t) -> p m t", t=512) for hf in range(2)]
        hb = sb("hb", [128, 8])
        s_hb = S("hb")
        P.add("vector", lambda e: e.tensor_scalar(out=hb[:], in0=pv[:, PV_GLB:PV_GLB + 8], scalar1=0.5, scalar2=None,
                                                  op0=ALU.mult), reads=[s_pv], writes=[s_hb])
        load_gin(0)
    def glu_part(q):
        cb = q % 2
        if stage >= 5:
            for m2 in range(8):
                b = m2 % 2
                P.add("tensor", lambda e, b=b, m2=m2: [e.matmul(psB[b][:, :], lhsT=wglu_r[:, m2, k, :], rhs=gin[:, k, :],
                                                                start=(k == 0), stop=(k == 7)) for k in range(8)][-1],
                      reads=[s_wgr_l[m2], s_gin], writes=[s_psB[b]])
                P.add("scalar", lambda e, b=b, m2=m2: e.activation(out=sg2, in_=psB[b][:], func=AF.Tanh, scale=0.5,
                                                                 bias=hb[:, m2:m2 + 1]),
                      reads=[s_psB[b], s_hb], writes=[s_sg2])
                P.add("vector", lambda e, m2=m2: e.scalar_tensor_tensor(out=sso[:, m2, :], in0=sg2, scalar=1.0,
                                                                        in1=gin[:, m2, :], op0=ALU.add, op1=ALU.mult),
                      reads=[s_gin, s_sg2], writes=[s_sso])
                P.add("vector", lambda e, m2=m2: e.scalar_tensor_tensor(out=sso[:, m2, :], in0=sso[:, m2, :], scalar=0.5,
                                                                        in1=sgin[m2 // 4][:, m2 % 4, :], op0=ALU.mult,
                                                                        op1=ALU.mult),
                      reads=[s_sso, s_wom[m2 // 4]], writes=[s_sso])
    def load_cin(q):
        cb = q % 2
        P.add("gpsimd", lambda e, q=q: e.dma_start(
            out=cin1, in_=conv_scr[:, :, q * 512:(q + 1) * 512].rearrange("j p t -> p j t")),
            reads=[s_scr[j][q] for j in range(8)], writes=[s_dg], dma=True)
    def outproj(q):
        cb = q % 2
        for m in range(8):
            b = m % 2

            if stage < 5:
                load_wo(m, b)

            def mo(e, m=m, b=b, cb=cb, q=q):
                ins = None
                for k in range(K_TILES):
                    rhs = cin1[:, k, :] if k < 8 else sso[:, k - 8, :]
                    ins = e.matmul(psA[b][:, :], lhsT=(wom[b][:, k, :] if stage < 5 else wo_r[:, m, k, :]), rhs=rhs,
                                   start=(k == 0), stop=(k == K_TILES - 1))
                return ins
            rd = ([s_wom[b], s_dg] if stage < 5 else [s_wor_l[m], s_dg, s_sso])
            P.add("tensor", mo, reads=rd, writes=[s_psA[b]])
            P.add("vector", lambda e, m=m, b=b: e.tensor_scalar(out=mixg[:, m, :], in0=psA[b][:], scalar1=ada[:, 16 + m, 0:1],
                                                               scalar2=None, op0=ALU.mult),
                  reads=[s_psA[b], s_ada], writes=[s_mixg])
    def tail_part(q):
        cb = q % 2
        for pb in range(4):
            gi = q * 4 + pb
            tb = gi % 2
            p0 = q * 512 + pb * 128
            s_i, c0 = p0 // NCH, p0 % NCH
            ncb = min(128, NCH)
            runs = [(s_i + r, c0 if NCH >= 128 else 0, ncb) for r in range(128 // ncb)]
            for ri, (ss, cc, n) in enumerate(runs):
                P.add("gpsimd", lambda e, tb=tb, ss=ss, cc=cc, n=n, ri=ri: e.dma_start(
                    out=xres[tb][ri * n:(ri + 1) * n, :], in_=x_sc[ss, cc:cc + n, :]),
                    writes=[s_xres[tb]], dma=True)

            def trf(e, tb=tb, pb=pb):
                ins = None
                for m in range(8):
                    ins = e.transpose(out=psT[tb][:, m * 128:(m + 1) * 128], in_=mixg[:, m, pb * 128:(pb + 1) * 128],
                                      identity=idf[:])
                return ins
            P.add("tensor", trf, reads=[s_mixg, s_idf], writes=[s_pstr[tb]])
            P.add("vector", lambda e, tb=tb: e.tensor_tensor(out=xo[tb], in0=psT[tb], in1=xres[tb][:], op=ALU.add),
                  reads=[s_pstr[tb], s_xres[tb]], writes=s_xo[tb])
            P.add("scalar", lambda e, tb=tb: e.activation(out=junk[:], in_=xo[tb], func=AF.Square,
                                                         accum_out=ssq[tb][:]),
                  reads=s_xo[tb], writes=[s_junk, s_ssq[tb]])
            P.add("scalar", lambda e, tb=tb: e.activation(out=rstd[tb][:], in_=ssq[tb][:], func=AF.Sqrt,
                                                         scale=1.0 / D, bias=EPS),
                  reads=[s_ssq[tb]], writes=[s_rstd[tb]])
            P.add("vector", lambda e, tb=tb: e.reciprocal(out=rstd[tb][:], in_=rstd[tb][:]),
                  reads=[s_rstd[tb]], writes=[s_rstd[tb]])
            P.add("vector", lambda e, tb=tb: e.scalar_tensor_tensor(out=xo[tb], in0=xo[tb], scalar=rstd[tb][:, 0:1],
                                                                  in1=fg, op0=ALU.mult, op1=ALU.mult),
                  reads=s_xo[tb] + [s_rstd[tb], s_xn[0], s_xn[1]], writes=s_xo[tb])
            for ri, (ss, cc, n) in enumerate(runs):
                stores.append(P.add("sync", lambda e, tb=tb, ss=ss, cc=cc, n=n, ri=ri: e.dma_start(
                    out=y_sc[ss, cc:cc + n, :], in_=xo[tb][ri * n:(ri + 1) * n, :]),
                    reads=s_xo[tb], dma=True))
    load_cin(0)
    if stage >= 5:
        glu_part(0)
    for q in range(NT):
        if stage >= 5 and q + 1 < NT:
            load_gin(q + 1)
        outproj(q)
        if q + 1 < NT:
            load_cin(q + 1)
            if stage >= 5:
                glu_part(q + 1)
        tail_part(q)
    P.add("sync", lambda e: e.nop(), reads=[], writes=[], force=True).deps.update(o.idx for o in stores if o.idx >= 0)
    P.emit(nc)
    es.close()
    return nc


def _col8(v):
    return np.ascontiguousarray(np.asarray(v, np.float32).reshape(-1, 128).T)


def prep_shared(inp, L_=4096, CTX_=256):
    f = lambda k: np.asarray(inp[k], np.float32)
    sh = {}
    sh["ident"] = np.eye(128, dtype=np.float32)
    sh["w_ada"] = np.ascontiguousarray(f("w_ada")[0])
    sh["w_in"] = np.ascontiguousarray(f("w_in")[0])
    sh["w_out"] = np.ascontiguousarray(f("w_out")[0])
    sh["glu_w"] = np.ascontiguousarray(f("ssm_glu_w")[0])
    NF_ = (L_ + CTX_) // 8
    kk = np.zeros((16, 25), np.float32)
    sidx = np.arange(8, dtype=np.float32)
    for e in range(16):
        if e < 8:
            kk[e, 0:8] = 7 - sidx; kk[e, 8:16] = -(sidx + 1); kk[e, 16:24] = sidx + 1
        else:
            kk[e, 0:8] = sidx; kk[e, 8:16] = sidx - 8; kk[e, 16:24] = 8 - sidx
        kk[e, 24] = 1.0
    sh["kk"] = np.ascontiguousarray(np.broadcast_to(kk.reshape(1, 400), (128, 400)))
    sg = np.ones((128, 2), np.float32); sg[64:, 0] = -1.0; sg[:, 1] = -sg[:, 0]
    sh["sgn"] = sg
    ix = np.arange(NF_, dtype=np.float32)
    sh["idx2"] = np.ascontiguousarray(np.broadcast_to(np.concatenate([ix, ix[::-1]])[None, :], (128, 2 * NF_)))
    sp = np.repeat(np.arange(8), 16)
    mf = (sp[None, :] >= sp[:, None]).astype(np.float32)
    sh["msk"] = np.ascontiguousarray(np.concatenate([mf, mf.T], axis=1))
    dd = f("ssm_d")[0].reshape(64, 16)
    sh["dcol"] = np.ascontiguousarray(np.tile(dd.T, (8, 1)))
    n2 = np.arange(128) % 64
    top = (np.arange(128) < 64)
    are = f("ssm_a_re")[0]; aim = f("ssm_a_im")[0]; ldt = f("ssm_log_dt")[0]
    bre = f("ssm_b_re")[0]; bim = f("ssm_b_im")[0]
    cre = f("ssm_c_re")[0]; cim = f("ssm_c_im")[0]
    sc = np.zeros((8, 128, 48), np.float32)
    bx = np.zeros((8, 128, 2, 16, 16), np.float32)
    cx = np.zeros((8, 128, 2, 16, 16), np.float32)
    for mt in range(8):
        for d in range(2):
            for gl in range(8):
                g = 8 * mt + gl; e = d * 8 + gl
                sc[mt, :, e] = are[d, g, n2]
                sc[mt, :, 16 + e] = aim[d, g, n2]
                sc[mt, :, 32 + e] = ldt[d, g]
                br = bre[d, g][n2, :]; bi = bim[d, g][n2, :]
                bx[mt, :, 0, e, :] = np.where(top[:, None], br, bi)
                bx[mt, :, 1, e, :] = np.where(top[:, None], bi, br)
                cr = cre[d, g].T[n2, :]; ci = cim[d, g].T[n2, :]
                cx[mt, :, 0, e, :] = np.where(top[:, None], cr, ci)
                cx[mt, :, 1, e, :] = np.where(top[:, None], ci, cr)
    sh["ssm_sc"] = sc
    sh["ssm_bx"] = bx.reshape(8, 128, 512)
    sh["ssm_cx"] = cx.reshape(8, 128, 512)
    sh["fg_b"] = np.ascontiguousarray(np.broadcast_to(f("final_g")[None, :], (128, D)))
    dw = f("conv_dw")[0]
    dwl = dw.T.reshape(8, 128, 31).transpose(1, 0, 2).reshape(128, 248)
    sh["_pv_tail"] = np.concatenate([
        _col8(f("c_ctx")), _col8(f("norm_g")[0]), _col8(f("conv_db")[0]), _col8(f("conv_ln_g")[0]),
        _col8(f("conv_ln_b")[0]), _col8(f("ssm_d")[0]), _col8(f("ssm_glu_b")[0]), _col8(f("b_ada")[0]), dwl], axis=1)
    return sh


def prep_core(inp, b, L, CTX, sh=None):
    if sh is None:
        sh = prep_shared(inp, L, CTX)
    m = {k: v for k, v in sh.items() if not k.startswith("_")}
    m["x"] = np.ascontiguousarray(np.asarray(inp["x"], np.float32)[b, :L])
    m["ctx"] = np.ascontiguousarray(np.asarray(inp["ctx"], np.float32)[b, :CTX])
    m["pv"] = np.ascontiguousarray(np.concatenate([_col8(np.asarray(inp["c"], np.float32)[b]), sh["_pv_tail"]], axis=1))
    return m


def kernel(**inputs):
    L, CTX, NB = 4096, 256, 8
    sh = prep_shared(inputs, L, CTX)
    in_maps = [prep_core(inputs, b, L, CTX, sh) for b in range(NB)]
    nc = bass.Bass("TRN2", target_bir_lowering=False)
    build(nc, L, CTX, stage=5)
    res = run_bass_kernel_spmd(nc, in_maps, core_ids=list(range(NB)))
    return np.stack([np.asarray(r["y"], np.float32) for r in res.results], axis=0)
```

```python
import math
import numpy as np
import concourse.bass as bass
import concourse.mybir as mybir
from concourse.bass_utils import run_bass_kernel_spmd

F32 = mybir.dt.float32
BF16 = mybir.dt.bfloat16
I32 = mybir.dt.int32
ALU = mybir.AluOpType
AF = mybir.ActivationFunctionType

D = 1024
EPS = 1e-6
TWO_PI = 2.0 * math.pi


class Slot:
    __slots__ = ("name", "last_w", "readers", "excl")

    def __init__(self, name, excl=False):
        self.name = name
        self.last_w = None
        self.readers = []
        self.excl = excl


class Op:
    __slots__ = ("eng", "fn", "deps", "idx", "dma", "sem", "val", "needed", "prewait")

    def __init__(self, eng, fn, dma):
        self.eng = eng
        self.fn = fn
        self.dma = dma
        self.deps = set()
        self.sem = None
        self.val = None
        self.needed = False
        self.prewait = None


class Prog:
    ENGS = ("sync", "scalar", "vector", "gpsimd", "tensor")

    def __init__(self):
        self.ops = []

    limit = None
    _cap = None

    def capture(self):
        self._cap = []
        return self._cap

    def end_capture(self):
        c = self._cap
        self._cap = None
        return c

    def add_merged(self, *lists):
        its = [list(l) for l in lists]
        while any(its):
            for l in its:
                if l:
                    a = l.pop(0)
                    self.add(*a[:2], reads=a[2], writes=a[3], dma=a[4])

    def add(self, eng, fn, reads=(), writes=(), dma=False, force=False):
        if self._cap is not None:
            self._cap.append((eng, fn, list(reads), list(writes), dma))
            return None
        op = Op(eng, fn, dma)
        if self.limit is not None and len(self.ops) >= self.limit and not force:
            op.idx = -1
            return op
        op.idx = len(self.ops)
        for s in reads:
            if s.last_w is not None:
                op.deps.add(s.last_w)
            if s.excl:
                op.deps.update(s.readers)
        for s in writes:
            if s.last_w is not None:
                op.deps.add(s.last_w)
            op.deps.update(s.readers)
        for s in reads:
            s.readers.append(op.idx)
        for s in writes:
            s.last_w = op.idx
            s.readers = []
        op.deps.discard(op.idx)
        self.ops.append(op)
        return op

    def emit(self, nc, n_dma_sems=24):
        ops = self.ops
        for op in ops:
            if op.eng == "tensor" and not op.dma:
                op.deps = {d for d in op.deps if not (ops[d].eng == "tensor" and not ops[d].dma)}
            for d in op.deps:
                ops[d].needed = True
        import contextlib

        with contextlib.ExitStack() as es:
            esem = {e: es.enter_context(nc.semaphore("c_" + e)) for e in self.ENGS}
            dsem = {
                e: [es.enter_context(nc.semaphore("d_%s%d" % (e, i))) for i in range(n_dma_sems)]
                for e in ("sync", "gpsimd", "scalar")
            }
            ecount = {e: 0 for e in self.ENGS}
            dcount = {e: 0 for e in dsem}
            for op in ops:
                if op.dma:
                    i = dcount[op.eng]
                    dcount[op.eng] += 1
                    P = len(dsem[op.eng])
                    op.sem = dsem[op.eng][i % P]
                    op.val = 16 * (i // P + 1)
                    op.prewait = (op.sem, 16 * (i // P)) if i >= P else None
                elif op.needed:
                    ecount[op.eng] += 1
                    op.sem = esem[op.eng]
                    op.val = ecount[op.eng]
            block = es.enter_context(nc.Block())

            def make(ename):
                def body(eng):
                    waited = {}
                    for op in ops:
                        if op.eng != ename:
                            continue
                        wl = []
                        if op.prewait is not None:
                            wl.append(op.prewait)
                        for d in sorted(op.deps):
                            wl.append((ops[d].sem, ops[d].val))
                        for sem, val in wl:
                            k = id(sem)
                            if waited.get(k, 0) >= val:
                                continue
                            waited[k] = val
                            eng.wait_ge(sem, val)
                        ins = op.fn(eng)
                        if op.dma:
                            ins.then_inc(op.sem, 16)
                        elif op.needed:
                            ins.then_inc(op.sem, 1)

                return body

            for e in self.ENGS:
                getattr(block, e)(make(e))


def bcast(ap, n, axis=None):
    dims = [list(d) for d in ap.ap]
    if axis is None:
        dims.append([0, n])
    else:
        dims.insert(axis, [0, n])
    return bass.AP(ap.tensor, ap.offset, dims)


def rawap(ap, dims):
    return bass.AP(ap.tensor, ap.offset, [list(ap.ap[0])] + [list(d) for d in dims])


PV_C, PV_CC, PV_NG, PV_CDB, PV_LNG, PV_LNB, PV_SD, PV_GLB = [8 * i for i in range(8)]
PV_BADA = 64
PV_DW = 88
PV_N = 88 + 248


def build(nc, L=4096, CTX=256, stage=99, dbg=()):
    import contextlib

    NT = L // 512
    NCH = L // 8
    CCH = CTX // 8
    TOK = CTX + L
    P = Prog()
    import os
    if os.environ.get('OPLIMIT'):
        P.limit = int(os.environ['OPLIMIT'])
    es = contextlib.ExitStack()
    dram = lambda n, s, d=F32, k="ExternalInput": nc.dram_tensor(n, s, d, kind=k).ap()
    x_d = dram("x", [L, D])
    ctx_d = dram("ctx", [CTX, D])
    pv_d = dram("pv", [128, PV_N])
    idf_d = dram("ident", [128, 128])
    wada_d = dram("w_ada", [D, 3 * D])
    win_d = dram("w_in", [D, 5 * D])
    y_d = dram("y", [L, D], F32, "ExternalOutput")

    sb = lambda n, s, d=F32: es.enter_context(nc.sbuf_tensor("sb_" + n, s, d))
    pst = lambda n, s, d=F32: es.enter_context(nc.psum_tensor("ps_" + n, s, d))
    S = lambda n: Slot(n)

    pv = sb("pv", [128, PV_N]); s_pv = S("pv")
    idf = sb("idf", [128, 128]); s_idf = S("idf")
    idb = sb("idb", [128, 128], BF16); s_idb = S("idb")
    P.add("sync", lambda e: e.dma_start(out=pv[:], in_=pv_d[:, :]), writes=[s_pv], dma=True)
    P.add("sync", lambda e: e.dma_start(out=idf[:], in_=idf_d[:, :]), writes=[s_idf], dma=True)
    P.add("vector", lambda e: e.tensor_copy(out=idb[:], in_=idf[:]), reads=[s_idf], writes=[s_idb])

    sil = sb("sil", [128, 8, 2]); s_sil = S("sil")
    P.add("scalar", lambda e: e.activation(out=sil[:, :, 0], in_=pv[:, PV_C:PV_C + 8], func=AF.Silu),
          reads=[s_pv], writes=[s_sil])
    P.add("scalar", lambda e: e.activation(out=sil[:, :, 1], in_=pv[:, PV_CC:PV_CC + 8], func=AF.Silu),
          reads=[s_pv, s_sil], writes=[s_sil])
    vpad = sb("vpad", [128, 8192], BF16); s_v = S("vpad")
    wa_v = vpad[:].bitcast(F32)
    wa = [wa_v[:, i * 2048:(i + 1) * 2048].rearrange("p (k n) -> p k n", n=256) for i in range(2)]
    s_wa = [S("wa%d" % i) for i in range(2)]
    psA = [pst("psA%d" % i, [128, 512]) for i in range(2)]; s_psA = [Slot("psA%d" % i, True) for i in range(2)]
    psB = [pst("psB%d" % i, [128, 512]) for i in range(2)]; s_psB = [Slot("psB%d" % i, True) for i in range(2)]
    ps_ada = psB[1][:, 0:48]; s_psada = s_psB[1]
    wada_v = wada_d.rearrange("(kt p) n -> p kt n", p=128)
    for ch in range(12):
        b = ch % 2
        P.add("sync", lambda e, b=b, ch=ch: e.dma_start(out=wa[b], in_=wada_v[:, :, ch * 256:(ch + 1) * 256]),
              writes=[s_wa[b]], dma=True)

        def mm(e, b=b, ch=ch):
            ins = None
            for m4 in range(2):
                m = ch * 2 + m4
                for kt in range(8):
                    ins = e.matmul(psB[1][:, 2 * m:2 * m + 2], lhsT=wa[b][:, kt, m4 * 128:(m4 + 1) * 128],
                                   rhs=sil[:, kt, :], start=(kt == 0), stop=(kt == 7))
            return ins
        P.add("tensor", mm, reads=[s_wa[b], s_sil], writes=[s_psada])
    ada = sb("ada", [128, 24, 2]); s_ada = S("ada")
    P.add("vector", lambda e: e.tensor_tensor(out=ada[:], in0=psB[1][:, 0:48].rearrange("p (m v) -> p m v", v=2),
                                              in1=bcast(pv[:, PV_BADA:PV_BADA + 24], 2), op=ALU.add),
          reads=[s_psada, s_pv], writes=[s_ada])
    sc1 = sb("sc1", [128, 8, 2]); s_sc1 = S("sc1")
    P.add("vector", lambda e: e.scalar_tensor_tensor(out=sc1[:], in0=ada[:, 8:16, :], scalar=1.0,
                                                     in1=bcast(pv[:, PV_NG:PV_NG + 8], 2), op0=ALU.add, op1=ALU.mult),
          reads=[s_ada, s_pv], writes=[s_sc1])

    h_fm = sb("h_fm", [128, 8, TOK], BF16)
    s_h = [S("h%d" % i) for i in range(TOK // 256)]
    NXB = 3
    xt_all = sb("xt_all", [128, NXB, D]); xt = [xt_all[:, i, :] for i in range(NXB)]; s_xt = [S("xt%d" % i) for i in range(NXB)]
    xn_all = sb("xn_all", [128, 2, D], BF16); xn = [xn_all[:, i, :] for i in range(2)]; s_xn = [S("xn%d" % i) for i in range(2)]
    junk = sb("junk", [128, D], BF16); s_junk = S("junk")
    ssq = [sb("ssq%d" % i, [128, 1]) for i in range(2)]; s_ssq = [S("ssq%d" % i) for i in range(2)]
    rstd = [sb("rstd%d" % i, [128, 1]) for i in range(2)]; s_rstd = [S("rstd%d" % i) for i in range(2)]
    ps_tr = [pst("ps_tr%d" % i, [128, 8, 256], BF16) for i in range(2)]; s_pstr = [Slot("ps_tr%d" % i, True) for i in range(2)]
    tiles = [(ctx_d, i * 128, 1) for i in range(CTX // 128)] + [(x_d, i * 128, 0) for i in range(L // 128)]
    for ti, (src, r0, v) in enumerate(tiles):
        xb = ti % NXB; nb = ti % 2; g = ti // 2; pb = g % 2; half = ti % 2
        P.add("sync", lambda e, src=src, r0=r0, xb=xb: e.dma_start(out=xt[xb][:], in_=src[r0:r0 + 128, :]),
              writes=[s_xt[xb]], dma=True)
        P.add("scalar", lambda e, xb=xb, nb=nb: e.activation(out=junk[:], in_=xt[xb][:], func=AF.Square,
                                                           accum_out=ssq[nb][:]),
              reads=[s_xt[xb]], writes=[s_junk, s_ssq[nb]])
        P.add("scalar", lambda e, nb=nb: e.activation(out=rstd[nb][:], in_=ssq[nb][:], func=AF.Sqrt,
                                                     scale=1.0 / D, bias=EPS),
              reads=[s_ssq[nb]], writes=[s_rstd[nb]])
        P.add("vector", lambda e, nb=nb: e.reciprocal(out=rstd[nb][:], in_=rstd[nb][:]),
              reads=[s_rstd[nb]], writes=[s_rstd[nb]])
        P.add("vector", lambda e, xb=xb, nb=nb: e.tensor_scalar(out=xn[nb], in0=xt[xb][:], scalar1=rstd[nb][:, 0:1],
                                                              scalar2=None, op0=ALU.mult),
              reads=[s_xt[xb], s_rstd[nb]], writes=[s_xn[nb]])

        def tr(e, nb=nb, pb=pb, half=half):
            ins = None
            for kt in range(8):
                ins = e.transpose(out=ps_tr[pb][:, kt, half * 128:(half + 1) * 128],
                                  in_=xn_all[:, nb, kt * 128:(kt + 1) * 128], identity=idb[:])
            return ins
        P.add("tensor", tr, reads=[s_xn[nb], s_idb], writes=[s_pstr[pb]])
        if half == 1:
            for kt in range(8):
                eng = "vector" if kt % 2 == 0 else "scalar"
                if eng == "vector":
                    f = lambda e, kt=kt, pb=pb, g=g, v=v: e.tensor_scalar(
                        out=h_fm[:, kt, g * 256:(g + 1) * 256], in0=ps_tr[pb][:, kt, :],
                        scalar1=sc1[:, kt, v:v + 1], scalar2=ada[:, kt, v:v + 1], op0=ALU.mult, op1=ALU.add)
                else:
                    f = lambda e, kt=kt, pb=pb, g=g, v=v: e.activation(
                        out=h_fm[:, kt, g * 256:(g + 1) * 256], in_=ps_tr[pb][:, kt, :], func=AF.Identity,
                        scale=sc1[:, kt, v:v + 1], bias=ada[:, kt, v:v + 1])
                P.add(eng, f, reads=[s_pstr[pb], s_sc1, s_ada], writes=[s_h[g]])

    stores = []
    if "h_fm" in dbg:
        dd = dram("dbg_h_fm", [128, 8 * TOK], BF16, "ExternalOutput")
        stores.append(P.add("sync", lambda e: e.dma_start(out=dd[:, :], in_=h_fm[:].rearrange("p k t -> p (k t)")),
                            reads=s_h, dma=True))
    if stage <= 1:
        P.add("sync", lambda e: e.nop(), reads=[], writes=[], force=True).deps.update(o.idx for o in stores if o.idx >= 0)
        P.emit(nc)
        es.close()
        return nc

    conv_scr = nc.dram_tensor("conv_scr", [8, 128, L], BF16, kind="Internal").ap()
    HALO = 15 * 64
    big = sb("big", [128, 8, L], BF16)
    s_big = [[S("big%d_%d" % (j, q)) for q in range(NT)] for j in range(8)]
    wst = [sb("wst%d" % i, [128, 8, 128]) for i in range(2)]; s_wst = [S("wst%d" % i) for i in range(2)]
    wb = [sb("wb%d" % i, [128, 8, 128], BF16) for i in range(2)]; s_wb = [S("wb%d" % i) for i in range(2)]
    dg = sb("dg", [128, 32, 128], BF16); s_dg = S("dg")
    sm = sb("sm", [128, 8, 512], BF16)
    cgq = sm[:, 2:6, :].rearrange("p a t -> p (a t)").rearrange("p (s c) -> p s c", c=256)
    s_cgq = [S("sqt0"), S("sqt1"), S("at0"), S("at1")]
    sgt = [sm[:, i, :] for i in range(2)]; s_sgt = [S("sgt%d" % i) for i in range(2)]
    win_v = win_d.rearrange("(kt p) n -> p kt n", p=128)
    wcnt = [0]

    def load_w(dsrc_v, col0, slot_i, dst=None, dslots=None, k0=0, ceng="gpsimd"):
        st = wcnt[0] % 2
        wcnt[0] += 1
        P.add("sync", lambda e: e.dma_start(out=wst[st][:], in_=dsrc_v[:, k0:k0 + 8, col0:col0 + 128]),
              writes=[s_wst[st]], dma=True)
        d_ = wb[slot_i][:] if dst is None else dst
        if ceng == "scalar":
            P.add("scalar", lambda e: e.activation(out=d_, in_=wst[st][:], func=AF.Copy),
                  reads=[s_wst[st]], writes=([s_wb[slot_i]] if dst is None else dslots))
        else:
            P.add(ceng, lambda e: e.tensor_copy(out=d_, in_=wst[st][:]),
                  reads=[s_wst[st]], writes=([s_wb[slot_i]] if dst is None else dslots))

    def proj(ps, s_ps, wslot, col0, ncols, colstep=1, wap=None, wslots=None):
        w_ = wb[wslot] if wap is None else wap
        def f(e):
            ins = None
            for kt in range(8):
                rhs = h_fm[:, kt, col0:col0 + ncols] if colstep == 1 else \
                    rawap(h_fm[:, kt, col0:col0 + 1], [[colstep, ncols]])
                ins = e.matmul(ps[:, 0:ncols], lhsT=w_[:, kt, :], rhs=rhs, start=(kt == 0), stop=(kt == 7))
            return ins
        hs = s_h if colstep != 1 else s_h[col0 // 256:(col0 + ncols - 1) // 256 + 1]
        P.add("tensor", f, reads=([s_wb[wslot]] if wap is None else wslots) + hs, writes=[s_ps])

    R_ = L // 64
    HP = 79
    cg_scr = nc.dram_tensor("cg_scr", [8, 128, L], BF16, kind="Internal").ap()
    s_cgscr = [S("cgs%d" % j) for j in range(8)]
    psT2 = [ps_tr[i][:].rearrange("p k t -> p (k t)").bitcast(F32) for i in range(2)]
    wom = [sb("wom%d" % i, [128, 16, 128], BF16) for i in range(2)]; s_wom = [S("wom%d" % i) for i in range(2)]
    for j in range(8):
        if j == 0 or j == 4:
            P.add("gpsimd", lambda e: e.memset(vpad[:], 0.0), writes=[s_v, s_wa[0], s_wa[1]])
        load_w(win_v, j * 128, 0)
        load_w(win_v, D + j * 128, 1)
        P.add("vector", lambda e, j=j: e.tensor_tensor(
            out=dg[:, 0:31, :], in0=bcast(idf[:], 31, axis=1),
            in1=bcast(pv[:, PV_DW + 31 * j:PV_DW + 31 * j + 31], 128), op=ALU.mult),
            reads=[s_idf, s_pv], writes=[s_dg])
        load_w(win_v, 2 * D + j * 128, 0, dst=wom[0][:, 0:8, :], dslots=[s_wom[0]])
        for nt in range(NT):
            b = nt % 2
            proj(psA[b], s_psA[b], 0, CTX + nt * 512, 512)
            proj(psB[b], s_psB[b], 1, CTX + nt * 512, 512)
            proj(psT2[b], s_pstr[b], 0, CTX + nt * 512, 512, wap=wom[0][:, 0:8, :], wslots=[s_wom[0]])
            c_in = (nt % 4) * 64
            P.add("scalar", lambda e, b=b, c_in=c_in: e.activation(
                out=rawap(cgq[:, 0, c_in:c_in + 1], [[256, 8], [1, 64]]),
                in_=rawap(psT2[b][:, 0:1], [[1, 8], [8, 64]]), func=AF.Silu), reads=[s_pstr[b]], writes=s_cgq)
            if nt % 4 == 3 or nt == NT - 1:
                w_ = c_in + 64
                c_out = (nt // 4) * 256
                P.add("gpsimd", lambda e, j=j, w_=w_, c_out=c_out: e.dma_start(
                    out=cg_scr[j].rearrange("p (s c) -> p s c", c=NCH)[:, :, c_out:c_out + w_], in_=cgq[:, :, 0:w_]),
                    reads=s_cgq, writes=[s_cgscr[j]], dma=True)
            P.add("scalar", lambda e, b=b: e.activation(out=sgt[b][:], in_=psB[b][:], func=AF.Tanh, scale=0.5),
                  reads=[s_psB[b]], writes=[s_sgt[b]])
            if j < 4:
                vo = rawap(vpad[:, nt * 8 * HP + 15:nt * 8 * HP + 16], [[HP, 8], [1, 64]])
                P.add("vector", lambda e, b=b, vo=vo: e.scalar_tensor_tensor(
                    out=vo, in0=sgt[b][:].rearrange("p (r w) -> p r w", w=64), scalar=1.0,
                    in1=psA[b][:].rearrange("p (r w) -> p r w", w=64), op0=ALU.add, op1=ALU.mult),
                    reads=[s_psA[b], s_sgt[b]], writes=[s_v])
            else:
                P.add("vector", lambda e, b=b, nt=nt: e.scalar_tensor_tensor(
                    out=vpad[:, HALO + nt * 512:HALO + (nt + 1) * 512], in0=sgt[b][:], scalar=1.0, in1=psA[b][:],
                    op0=ALU.add, op1=ALU.mult),
                    reads=[s_psA[b], s_sgt[b]], writes=[s_v])
        if j < 4:
            ctiles = [(r0, min(6, R_ - r0)) for r0 in range(0, R_, 6)]
        else:
            ctiles = [(r0, 8) for r0 in range(0, R_, 8)]
        for ci, (r0, nr) in enumerate(ctiles):
            b = ci % 2

            def cv(e, j=j, r0=r0, nr=nr, b=b):
                ins = None
                for tap in range(31):
                    d = tap - 15
                    if j < 4:
                        n = nr * HP
                        base = r0 * HP + 15 + d
                    else:
                        n = nr * 64
                        base = HALO + r0 * 64 + d * 64
                    ins = e.matmul(psA[b][:, 0:n], lhsT=dg[:, tap, :], rhs=vpad[:, base:base + n],
                                   start=(tap == 0), stop=(tap == 30))
                return ins
            P.add("tensor", cv, reads=[s_dg, s_v], writes=[s_psA[b]])
            wrow = HP if j < 4 else 64
            P.add("scalar", lambda e, j=j, r0=r0, nr=nr, b=b, wrow=wrow: e.activation(
                out=rawap(big[:, j, 8 * r0:8 * r0 + 1], [[NCH, 8], [8, nr], [1, 8]]),
                in_=rawap(psA[b][:, 0:1], [[1, 8], [wrow, nr], [8, 8]]), func=AF.Identity, scale=0.5,
                bias=pv[:, PV_CDB + j:PV_CDB + j + 1]),
                reads=[s_psA[b], s_pv], writes=[s_big[j][q] for q in range(NT)])
    if "conv_pre" in dbg:
        dd = dram("dbg_conv_pre", [128, 8 * L], BF16, "ExternalOutput")
        stores.append(P.add("sync", lambda e: e.dma_start(out=dd[:, :], in_=big[:].rearrange("p k t -> p (k t)")),
                            reads=[x for r in s_big for x in r], dma=True))
    if stage <= 2:
        P.add("sync", lambda e: e.nop(), reads=[], writes=[], force=True).deps.update(o.idx for o in stores if o.idx >= 0)
        P.emit(nc)
        es.close()
        return nc

    ones_b = sb("ones_b", [128, 128], BF16); s_ones = S("ones")
    P.add("gpsimd", lambda e: e.memset(ones_b[:], 1.0), writes=[s_ones])
    sqt = [sm[:, 2 + i, :] for i in range(2)]; s_sqt = s_cgq[0:2]
    fzA = sb("fzA", [128, 2])
    fenceA = P.add("gpsimd", lambda e: e.memset(fzA[:], 0.0), writes=s_xt)
    xf32 = xt_all[:].rearrange("p a d -> p (a d)")
    xb16 = xf32.bitcast(BF16)
    f_mean, f_msq, f_var = xf32[:, 0:512], xf32[:, 512:1024], xf32[:, 1024:1536]
    rstdb = [xb16[:, 3072 + 512 * i:3584 + 512 * i] for i in range(2)]
    nmrb = [xb16[:, 4096 + 512 * i:4608 + 512 * i] for i in range(2)]
    f_tb = [xb16[:, 5120 + 512 * i:5632 + 512 * i] for i in range(2)]
    s_2b = [S("ln%d" % i) for i in range(9)]
    for x_ in s_2b:
        x_.last_w = fenceA.idx
    s_mean, s_msq, s_var = s_2b[0:3]
    s_rsb, s_nmb, s_ftb = s_2b[3:5], s_2b[5:7], s_2b[7:9]
    at = [sm[:, 4 + i, :] for i in range(2)]; s_at = s_cgq[2:4]
    cot = [sm[:, 6 + i, :] for i in range(2)]; s_cot = [S("cot%d" % i) for i in range(2)]
    s_scr = [[S("scr%d_%d" % (j, q)) for q in range(NT)] for j in range(8)]

    def pos_cols(q):
        if NCH >= 512:
            s0 = (q * 512) // NCH
            c0 = (q * 512) % NCH
            return CTX + 8 * c0 + s0, [[8, 512]]
        ns = 512 // NCH
        s0 = q * ns
        return CTX + s0, [[1, ns], [8, NCH]]

    def proj_pos(ps, s_ps, wslot, q, wap=None, wslots=None):
        off, dims = pos_cols(q)
        w_ = wb[wslot] if wap is None else wap
        def f(e):
            ins = None
            for kt in range(8):
                ins = e.matmul(ps[:, :], lhsT=w_[:, kt, :], rhs=rawap(h_fm[:, kt, off:off + 1], dims),
                               start=(kt == 0), stop=(kt == 7))
            return ins
        P.add("tensor", f, reads=([s_wb[wslot]] if wap is None else wslots) + s_h, writes=[s_ps])

    cgin = dg[:].rearrange("p t c -> p (t c)").rearrange("p (j t) -> p j t", t=512)

    def ln_stats(q):
        sl = slice(q * 512, (q + 1) * 512); rs = q % 2
        for j in range(8):
            b = j % 2
            P.add("scalar", lambda e, j=j, b=b: e.activation(out=sqt[b][:], in_=big[:, j, sl], func=AF.Square),
                  reads=[s_big[j][q]], writes=[s_sqt[b]])
            P.add("tensor", lambda e, j=j: e.matmul(psA[0][:, :], lhsT=ones_b[:], rhs=big[:, j, sl],
                                                    start=(j == 0), stop=(j == 7)),
                  reads=[s_ones, s_big[j][q]], writes=[s_psA[0]])
            P.add("tensor", lambda e, j=j, b=b: e.matmul(psA[1][:, :], lhsT=ones_b[:], rhs=sqt[b][:],
                                                         start=(j == 0), stop=(j == 7)),
                  reads=[s_ones, s_sqt[b]], writes=[s_psA[1]])
        P.add("scalar", lambda e: e.activation(out=f_mean, in_=psA[0][:], func=AF.Copy, scale=1.0 / D),
              reads=[s_psA[0]], writes=[s_mean])
        P.add("scalar", lambda e: e.activation(out=f_msq, in_=psA[0][:], func=AF.Square, scale=1.0 / D),
              reads=[s_psA[0]], writes=[s_msq])
        P.add("vector", lambda e: e.scalar_tensor_tensor(out=f_var, in0=psA[1][:], scalar=1.0 / D, in1=f_msq,
                                                         op0=ALU.mult, op1=ALU.subtract),
              reads=[s_psA[1], s_msq], writes=[s_var])
        P.add("scalar", lambda e: e.activation(out=f_var, in_=f_var, func=AF.Sqrt, bias=EPS),
              reads=[s_var], writes=[s_var])
        P.add("vector", lambda e: e.reciprocal(out=f_var, in_=f_var), reads=[s_var], writes=[s_var])
        P.add("vector", lambda e: e.tensor_copy(out=rstdb[rs], in_=f_var), reads=[s_var], writes=[s_rsb[rs]])
        P.add("vector", lambda e: e.scalar_tensor_tensor(out=nmrb[rs], in0=f_mean, scalar=-1.0, in1=rstdb[rs],
                                                         op0=ALU.mult, op1=ALU.mult),
              reads=[s_mean, s_rsb[rs]], writes=[s_nmb[rs]])

    def ld_gate(g_):
        q_, j_ = divmod(g_, 8)
        P.add("gpsimd", lambda e: e.dma_start(out=sgt[g_ % 2][:], in_=cg_scr[j_, :, q_ * 512:(q_ + 1) * 512]),
              reads=[s_cgscr[j_]], writes=[s_sgt[g_ % 2]], dma=True)

    def ln_apply(q):
        sl = slice(q * 512, (q + 1) * 512); rs = q % 2

        def pre(j):
            b = j % 2
            P.add("vector", lambda e: e.tensor_tensor(out=f_tb[b], in0=big[:, j, sl], in1=rstdb[rs], op=ALU.mult),
                  reads=[s_big[j][q], s_rsb[rs]], writes=[s_ftb[b]])
            P.add("vector", lambda e: e.tensor_tensor(out=f_tb[b], in0=f_tb[b], in1=nmrb[rs], op=ALU.add),
                  reads=[s_ftb[b], s_nmb[rs]], writes=[s_ftb[b]])
            P.add("scalar", lambda e: e.activation(out=at[b][:], in_=f_tb[b], func=AF.Silu,
                                                   scale=pv[:, PV_LNG + j:PV_LNG + j + 1],
                                                   bias=pv[:, PV_LNB + j:PV_LNB + j + 1]),
                  reads=[s_ftb[b], s_pv], writes=[s_at[b]])

        def post(j):
            b = j % 2
            g_ = q * 8 + j
            P.add("vector", lambda e: e.tensor_tensor(out=cot[b][:], in0=at[b][:], in1=sgt[g_ % 2][:], op=ALU.mult),
                  reads=[s_at[b], s_sgt[g_ % 2]], writes=[s_cot[b]])
            P.add("sync", lambda e: e.dma_start(out=conv_scr[j, :, sl], in_=cot[b][:]),
                  reads=[s_cot[b]], writes=[s_scr[j][q]], dma=True)
            if g_ + 2 < 8 * NT:
                ld_gate(g_ + 2)

        pre(0)
        for j in range(8):
            if j + 1 < 8:
                pre(j + 1)
            post(j)

    ln_stats(0)
    ld_gate(0); ld_gate(1)
    for q in range(NT):
        la = []
        if q + 1 < NT:
            P.capture(); ln_stats(q + 1); la = P.end_capture()
        P.capture(); ln_apply(q); lb = P.end_capture()
        P.add_merged(lb, la)
    P.add("gpsimd", lambda e: e.memset(fzA[:], 0.0), writes=s_2b + s_xt)

    gy_scr = nc.dram_tensor("gy_scr", [8, 128, L], BF16, kind="Internal").ap()
    sg_scr = nc.dram_tensor("sg_scr", [8, 128, L], BF16, kind="Internal").ap()
    s_sgscr = [S("sgs%d" % mt) for mt in range(8)]
    s_gyscr = [[[S("gys%d_%d_%d" % (mt, gl, i)) for i in range(8)] for gl in range(8)] for mt in range(8)]
    if stage >= 4:
        NV = NCH + 2 * CCH
        NF = NCH + CCH
        MAGIC = 12582912.0
        sc_d = dram("ssm_sc", [8, 128, 48])
        bx_d = dram("ssm_bx", [8, 128, 512])
        cx_d = dram("ssm_cx", [8, 128, 512])
        kk_d = dram("kk", [128, 400])
        sgn_d = dram("sgn", [128, 2])
        idx_d = dram("idx2", [128, 2 * NF])
        msk_d = dram("msk", [128, 256])
        dcol_d = dram("dcol", [128, 64])
        fz = sb("fz", [128, 2])
        old_slots = [x for r in s_big for x in r] + s_xt + [s_dg] + s_sgt + s_sqt + s_at + s_cot + [s_v] + s_xn
        fence = P.add("gpsimd", lambda e: e.memset(fz[:], 0.0), writes=old_slots)
        if 16 * L >= 65536:
            arena = big[:].rearrange("p k t -> p (k t)")
        else:
            arena = sb("arena", [128, 32768], BF16)[:]
        aoff = [0]
        arena_slots = []

        arenas = {"main": (arena, 65536), "xt": (xt_all[:].rearrange("p a d -> p (a d)").bitcast(BF16), 12288),
                  "dg": (dg[:].rearrange("p a d -> p (a d)"), 8192), "sm": (sm[:].rearrange("p a d -> p (a d)"), 8192),
                  "vp": (vpad[:], 16384)}
        aoffs = {k: 0 for k in arenas}

        def carve(shape, dtype, name, where="main"):
            n = int(np.prod(shape))
            nb = n * (4 if dtype == F32 else 2)
            ar, cap = arenas[where]
            st = aoffs[where]
            aoffs[where] += (nb + 3) // 4 * 4
            aoff[0] = aoffs["main"]
            assert aoffs[where] <= cap, (where, aoffs[where])
            v = ar[:, st // 2:(st + nb) // 2]
            if dtype == F32:
                v = v.bitcast(F32)
            if len(shape) == 2:
                v = v.rearrange("p (a b) -> p a b", b=shape[1])
            elif len(shape) == 3:
                v = v.rearrange("p (a b c) -> p a b c", b=shape[1], c=shape[2])
            sl = Slot(name)
            sl.last_w = fence.idx
            arena_slots.append(sl)
            return v, sl

        kk, s_kk = carve([16, 25], F32, "kk")
        sgn, s_sgn = carve([2], F32, "sgn")
        idx2, s_idx = carve([2, NF], F32, "idx2")
        msk, s_msk = carve([2, 128], F32, "msk")
        dcol, s_dcol = carve([64], F32, "dcol")
        for dst, src, sl in ((kk, kk_d, s_kk), (sgn, sgn_d, s_sgn), (idx2, idx_d, s_idx), (msk, msk_d, s_msk),
                             (dcol, dcol_d, s_dcol)):
            P.add("sync", lambda e, dst=dst, src=src: e.dma_start(
                out=dst if len(dst.shape) == 2 else dst.rearrange("p a b -> p (a b)"), in_=src[:, :]),
                writes=[sl], dma=True)
        sc_t, s_sc = carve([48], F32, "sc_t")
        bx_t, s_bx = carve([2, 16, 16], F32, "bx_t")
        cx_t, s_cx = carve([2, 16, 16], F32, "cx_t")
        NPM = 24
        pm, _ = carve([NPM, 16], F32, "pm")
        s_pm = [Slot("pm%d" % i) for i in range(NPM)]
        for x in s_pm:
            x.last_w = fence.idx
        arena_slots.extend(s_pm)
        (R_DT, R_MAG, R_ANG, R_A, R_P1, R_P2, R_NR, R_NI, R_DEN, R_CR, R_CI, R_CIA, R_CIB, R_RHO, R_F8, R_U, R_K,
         R_FCH) = range(18)
        NTB = 7
        tbl, _ = carve([NTB, 16, 25], F32, "tbl")
        s_tb = [Slot("tbl%d" % i) for i in range(NTB)]
        for x in s_tb:
            x.last_w = fence.idx
        arena_slots.extend(s_tb)
        (T_T, T_U, T_EM, T_SIN, T_COS, T_LR, T_LI) = range(7)
        T_LRX, T_LIS, T_NEG = T_SIN, T_COS, T_EM
        bb, s_bb = carve([2, 16, 16], F32, "bb")
        gtmp, s_gtmp = carve([2, 8, 16], F32, "gtmp")
        Uq, s_uq = carve([8, NV], BF16, "Uq")
        Vt = []; s_vt = []
        for gl in range(8):
            a, b_ = carve([NV], BF16, "V%d" % gl)
            sl8 = [b_] + [Slot("V%d_%d" % (gl, i)) for i in range(1, 8)]
            for x in sl8:
                x.last_w = fence.idx
            arena_slots.extend(sl8[1:])
            Vt.append(a); s_vt.append(sl8)
        Yq = []; s_yq = []
        for i in range(2):
            a, b_ = carve([NCH], BF16, "Yq%d" % i)
            Yq.append(a); s_yq.append(b_)
        gen = {}
        for nm in ("Ba0", "Bb0", "MT0", "Ba1", "Bb1", "MT1", "dgm"):
            gen[nm] = carve([8, 16], BF16, nm)
        GEN = {}
        for nm, wh in (("BaT", "xt"), ("BbT", "xt"), ("Q", "xt"), ("CT", "dg"), ("Cb", "dg")):
            GEN[nm] = carve([16, 128], BF16, "G" + nm, wh)
        gA, s_gA = carve([8, 8, 16], BF16, "gA", "sm")
        gB, s_gB = carve([8, 8, 16], BF16, "gB", "sm")
        tabs = {}
        for nm in ("t", "u", "fr", "SIN", "COS", "t1"):
            tabs[nm] = carve([NF], F32, nm)
        tabs["afr"] = tabs["u"]
        tabs["t2"] = tabs["t"]
        tabs["G"] = tabs["u"]
        Gc, s_gc = carve([NF], BF16, "Gc")
        Gs, s_gs = carve([NF], BF16, "Gs")
        ghy = sb("ghy", [128, 512], BF16); s_ghy = S("ghy"); s_gz = S("gz")
        sgq = xn_all[:].rearrange("p a d -> p (a d)").rearrange("p (s c) -> p s c", c=256)
        s_sgq = Slot("sgq"); s_sgq.last_w = fence.idx; arena_slots.append(s_sgq)
        TS = [dict(u=tabs["u"], fr=tabs["fr"], SIN=tabs["SIN"], COS=tabs["COS"], t1=tabs["t1"], t2=tabs["t"],
                   Gc=(Gc, s_gc), Gs=(Gs, s_gs))]
        t1b = {}
        for nm in ("t", "u", "fr", "SIN", "COS", "t1"):
            t1b[nm] = carve([NF], F32, nm + "_b", "vp")
        Gc1 = carve([NF], BF16, "Gc_b", "vp"); Gs1 = carve([NF], BF16, "Gs_b", "vp")
        TS.append(dict(u=t1b["u"], fr=t1b["fr"], SIN=t1b["SIN"], COS=t1b["COS"], t1=t1b["t1"], t2=t1b["t"], Gc=Gc1, Gs=Gs1))
        print("ssm arena bytes", aoff[0])

        PG = lambda f, r, w: P.add("gpsimd", f, reads=r, writes=w)
        PV_ = lambda f, r, w: P.add("vector", f, reads=r, writes=w)
        PA = lambda f, r, w: P.add("scalar", f, reads=r, writes=w)
        pmr = lambda i: pm[:, i, :]
        ZaP = ps_tr[0][:].rearrange("p k t -> p (k t)").bitcast(F32)
        ZbP = ps_tr[1][:].rearrange("p k t -> p (k t)").bitcast(F32)
        psBb = [psB[i][:].bitcast(BF16) for i in range(2)]
        pieces = [(0, min(512, NF))] + ([(512, NF)] if NF > 512 else [])

        def rev(ap2, n):
            d0 = list(ap2.ap[0]); d1 = list(ap2.ap[1])
            return bass.AP(ap2.tensor, ap2.offset + (n - 1) * d1[0], [d0, [-d1[0], n]])

        for mt in range(8):
            def load_params(m_):
                P.add("sync", lambda e: e.dma_start(out=sc_t, in_=sc_d[m_, :, :]), writes=[s_sc], dma=True)
                P.add("sync", lambda e: e.dma_start(out=bx_t.rearrange("p a b c -> p (a b c)"), in_=bx_d[m_, :, :]),
                      writes=[s_bx], dma=True)
                P.add("sync", lambda e: e.dma_start(out=cx_t.rearrange("p a b c -> p (a b c)"), in_=cx_d[m_, :, :]),
                      writes=[s_cx], dma=True)
            if mt == 0:
                load_params(0)
            are, aim, ldt = sc_t[:, 0:16], sc_t[:, 16:32], sc_t[:, 32:48]
            PA(lambda e: e.activation(out=pmr(R_DT), in_=ldt, func=AF.Exp), [s_sc], [s_pm[R_DT]])
            PV_(lambda e: e.tensor_tensor(out=pmr(R_MAG), in0=are, in1=pmr(R_DT), op=ALU.mult), [s_sc, s_pm[R_DT]], [s_pm[R_MAG]])
            PV_(lambda e: e.scalar_tensor_tensor(out=pmr(R_ANG), in0=aim, scalar=1.0 / TWO_PI, in1=pmr(R_DT),
                                                 op0=ALU.mult, op1=ALU.mult), [s_sc, s_pm[R_DT]], [s_pm[R_ANG]])
            T = lambda i: tbl[:, i, :, :]
            PV_(lambda e: e.tensor_tensor(out=T(T_EM), in0=bcast(pmr(R_MAG), 25), in1=kk, op=ALU.mult),
               [s_pm[R_MAG], s_kk], [s_tb[T_EM]])
            PA(lambda e: e.activation(out=T(T_EM), in_=T(T_EM), func=AF.Exp), [s_tb[T_EM]], [s_tb[T_EM]])
            PV_(lambda e: e.tensor_tensor(out=T(T_T), in0=bcast(pmr(R_ANG), 25), in1=kk, op=ALU.mult),
               [s_pm[R_ANG], s_kk], [s_tb[T_T]])
            PA(lambda e: e.activation(out=T(T_U).bitcast(I32), in_=T(T_T), func=AF.Identity), [s_tb[T_T]], [s_tb[T_U]])
            PV_(lambda e: e.tensor_tensor(out=T(T_T), in0=T(T_T), in1=T(T_U).bitcast(I32), op=ALU.subtract),
               [s_tb[T_T], s_tb[T_U]], [s_tb[T_T]])
            PA(lambda e: e.activation(out=T(T_U), in_=T(T_T), func=AF.Abs), [s_tb[T_T]], [s_tb[T_U]])
            PA(lambda e: e.activation(out=T(T_SIN), in_=T(T_T), func=AF.Sin, scale=TWO_PI), [s_tb[T_T]], [s_tb[T_SIN]])
            PA(lambda e: e.activation(out=T(T_COS), in_=T(T_U), func=AF.Sin, scale=-TWO_PI, bias=math.pi / 2),
               [s_tb[T_U]], [s_tb[T_COS]])
            PV_(lambda e: e.tensor_tensor(out=T(T_LR), in0=T(T_EM), in1=T(T_COS), op=ALU.mult),
               [s_tb[T_EM], s_tb[T_COS]], [s_tb[T_LR]])
            PV_(lambda e: e.tensor_tensor(out=T(T_LI), in0=T(T_EM), in1=T(T_SIN), op=ALU.mult),
               [s_tb[T_EM], s_tb[T_SIN]], [s_tb[T_LI]])
            PV_(lambda e: e.tensor_scalar(out=T(T_LRX), in0=T(T_LR), scalar1=sgn[:, 0:1], scalar2=None, op0=ALU.mult),
               [s_tb[T_LR], s_sgn], [s_tb[T_LRX]])
            PV_(lambda e: e.tensor_scalar(out=T(T_LIS), in0=T(T_LI), scalar1=sgn[:, 1:2], scalar2=None, op0=ALU.mult),
               [s_tb[T_LI], s_sgn], [s_tb[T_LIS]])
            PV_(lambda e: e.tensor_scalar(out=T(T_NEG), in0=T(T_LR), scalar1=-1.0, scalar2=None, op0=ALU.mult),
               [s_tb[T_LR]], [s_tb[T_NEG]])
            PV_(lambda e: e.tensor_scalar(out=T(T_U), in0=T(T_LI), scalar1=-1.0, scalar2=None, op0=ALU.mult),
               [s_tb[T_LI]], [s_tb[T_U]])
            LRd, LRx, LRn, LIs, LIy, LIn = T_LR, T_LRX, T_NEG, T_LIS, T_LI, T_U
            lbr, lbi = tbl[:, T_LR, :, 24], tbl[:, T_LI, :, 24]
            TT = lambda o, a, b_, op, rd, wr: PV_(lambda e: e.tensor_tensor(out=o, in0=a, in1=b_, op=op), rd, wr)
            PV_(lambda e: e.tensor_scalar(out=pmr(R_A), in0=lbr, scalar1=-1.0, scalar2=None, op0=ALU.add),
               [s_tb[T_LR]], [s_pm[R_A]])
            TT(pmr(R_P1), pmr(R_A), are, ALU.mult, [s_pm[R_A], s_sc], [s_pm[R_P1]])
            TT(pmr(R_P2), lbi, aim, ALU.mult, [s_tb[T_LI], s_sc], [s_pm[R_P2]])
            TT(pmr(R_NR), pmr(R_P1), pmr(R_P2), ALU.add, [s_pm[R_P1], s_pm[R_P2]], [s_pm[R_NR]])
            TT(pmr(R_P1), lbi, are, ALU.mult, [s_tb[T_LI], s_sc, s_pm[R_NR]], [s_pm[R_P1]])
            TT(pmr(R_P2), pmr(R_A), aim, ALU.mult, [s_pm[R_A], s_sc, s_pm[R_NR]], [s_pm[R_P2]])
            TT(pmr(R_NI), pmr(R_P1), pmr(R_P2), ALU.subtract, [s_pm[R_P1], s_pm[R_P2]], [s_pm[R_NI]])
            TT(pmr(R_P1), are, are, ALU.mult, [s_sc, s_pm[R_NI]], [s_pm[R_P1]])
            TT(pmr(R_P2), aim, aim, ALU.mult, [s_sc, s_pm[R_NI]], [s_pm[R_P2]])
            TT(pmr(R_DEN), pmr(R_P1), pmr(R_P2), ALU.add, [s_pm[R_P1], s_pm[R_P2]], [s_pm[R_DEN]])
            PV_(lambda e: e.reciprocal(out=pmr(R_DEN), in_=pmr(R_DEN)), [s_pm[R_DEN]], [s_pm[R_DEN]])
            TT(pmr(R_CR), pmr(R_NR), pmr(R_DEN), ALU.mult, [s_pm[R_NR], s_pm[R_DEN]], [s_pm[R_CR]])
            TT(pmr(R_CI), pmr(R_NI), pmr(R_DEN), ALU.mult, [s_pm[R_NI], s_pm[R_DEN]], [s_pm[R_CI]])
            PV_(lambda e: e.tensor_scalar(out=pmr(R_CIA), in0=pmr(R_CI), scalar1=sgn[:, 1:2], scalar2=None, op0=ALU.mult),
               [s_pm[R_CI], s_sgn], [s_pm[R_CIA]])
            PV_(lambda e: e.tensor_scalar(out=pmr(R_CIB), in0=pmr(R_CI), scalar1=sgn[:, 0:1], scalar2=None, op0=ALU.mult),
               [s_pm[R_CI], s_sgn], [s_pm[R_CIB]])
            X1, X2 = bx_t[:, 0, :, :], bx_t[:, 1, :, :]
            g0 = gtmp.rearrange("p a b c -> p (a b c)")[:, 0:256].rearrange("p (a b) -> p a b", b=16)
            TT(bb[:, 0, :, :], bcast(pmr(R_CR), 16), X1, ALU.mult, [s_pm[R_CR], s_bx], [s_bb])
            TT(g0, bcast(pmr(R_CIA), 16), X2, ALU.mult, [s_pm[R_CIA], s_bx], [s_gtmp])
            TT(bb[:, 0, :, :], bb[:, 0, :, :], g0, ALU.add, [s_bb, s_gtmp], [s_bb])
            TT(bb[:, 1, :, :], bcast(pmr(R_CR), 16), X2, ALU.mult, [s_pm[R_CR], s_bx, s_bb], [s_bb])
            TT(g0, bcast(pmr(R_CIB), 16), X1, ALU.mult, [s_pm[R_CIB], s_bx, s_bb], [s_gtmp])
            TT(bb[:, 1, :, :], bb[:, 1, :, :], g0, ALU.add, [s_bb, s_gtmp], [s_bb])
            PA(lambda e: e.activation(out=pmr(R_RHO), in_=pmr(R_MAG), func=AF.Exp, scale=8.0), [s_pm[R_MAG]], [s_pm[R_RHO]])
            PV_(lambda e: e.tensor_scalar(out=pmr(R_F8), in0=pmr(R_ANG), scalar1=8.0, scalar2=None, op0=ALU.mult),
               [s_pm[R_ANG]], [s_pm[R_F8]])
            PA(lambda e: e.activation(out=pmr(R_U).bitcast(I32), in_=pmr(R_F8), func=AF.Identity), [s_pm[R_F8]], [s_pm[R_U]])
            TT(pmr(R_FCH), pmr(R_F8), pmr(R_U).bitcast(I32), ALU.subtract, [s_pm[R_F8], s_pm[R_U]], [s_pm[R_FCH]])

            def realform_b(name, La, Wi, Lb, Wj, koff, Wt, s_w):
                o, so = GEN[name]
                for hf in range(2):
                    es_ = slice(8 * hf, 8 * hf + 8)
                    la = bcast(tbl[:, La, es_, koff:koff + 8], 16)
                    lb = bcast(tbl[:, Lb, es_, koff:koff + 8], 16)
                    wa_ = bcast(Wt[:, Wi, es_, :], 8, axis=2)
                    wb_ = bcast(Wt[:, Wj, es_, :], 8, axis=2)
                    oo = o[:, es_, :].rearrange("p e (s h) -> p e s h", h=16)
                    PV_(lambda e, la=la, wa_=wa_: e.tensor_tensor(out=gA, in0=la, in1=wa_, op=ALU.mult),
                        [s_tb[La], s_w], [s_gA])
                    PV_(lambda e, lb=lb, wb_=wb_: e.tensor_tensor(out=gB, in0=lb, in1=wb_, op=ALU.mult),
                        [s_tb[Lb], s_w], [s_gB])
                    PV_(lambda e, oo=oo: e.tensor_tensor(out=oo, in0=gA, in1=gB, op=ALU.add), [s_gA, s_gB], [so])
            realform_b("BaT", LRd, 0, LIs, 1, 0, bb, s_bb)
            realform_b("BbT", LRx, 1, LIy, 0, 0, bb, s_bb)
            realform_b("Q", LRd, 0, LIs, 1, 8, bb, s_bb)
            realform_b("CT", LRx, 0, LIn, 1, 16, cx_t, s_cx)
            realform_b("Cb", LRn, 1, LIs, 0, 16, cx_t, s_cx)
            if mt == 0:
                load_w(win_v, 3 * D + mt * 128, 0)
            proj(psA[0], s_psA[0], 0, 0, CTX)
            for c0 in (0, CCH + NCH):
                PA(lambda e, c0=c0: e.activation(out=rawap(Uq[:, 0, c0:c0 + 1], [[NV, 8], [1, CCH]]),
                                                 in_=rawap(psA[0][:, 0:1], [[1, 8], [8, CCH]]), func=AF.Copy),
                   [s_psA[0]], [s_uq])
            for nt in range(NT):
                b = (nt + 1) % 2
                proj(psA[b], s_psA[b], 0, CTX + nt * 512, 512)
                PA(lambda e, nt=nt, b=b: e.activation(
                    out=rawap(Uq[:, 0, CCH + 64 * nt:CCH + 64 * nt + 1], [[NV, 8], [1, 64]]),
                    in_=rawap(psA[b][:, 0:1], [[1, 8], [8, 64]]), func=AF.Copy), [s_psA[b]], [s_uq])
            if mt == 0:
                load_w(win_v, 4 * D + mt * 128, 1)
            TPH = 4
            for nt in range(NT):
                b = nt % 2
                proj(psA[b], s_psA[b], 1, CTX + nt * 512, 512)
                c_in = (nt % TPH) * 64
                PA(lambda e, b=b, c_in=c_in: e.activation(
                    out=rawap(sgq[:, 0, c_in:c_in + 1], [[256, 8], [1, 64]]),
                    in_=rawap(psA[b][:, 0:1], [[1, 8], [8, 64]]), func=AF.Silu), [s_psA[b]], [s_sgq])
                if nt % TPH == TPH - 1 or nt == NT - 1:
                    w_ = c_in + 64
                    c_out = (nt // TPH) * 256
                    P.add("gpsimd", lambda e, mt=mt, w_=w_, c_out=c_out: e.dma_start(
                        out=sg_scr[mt].rearrange("p (s c) -> p s c", c=NCH)[:, :, c_out:c_out + w_], in_=sgq[:, :, 0:w_]),
                        reads=[s_sgq], writes=[s_sgscr[mt]], dma=True)
            for gl in range(8):
                for s_ in range(8):
                    P.add("sync" if s_ % 2 == 0 else "gpsimd", lambda e, gl=gl, s_=s_: e.dma_start(
                        out=Vt[gl][16 * s_:16 * s_ + 16, :], in_=Uq[16 * gl:16 * gl + 16, s_, :]),
                        reads=[s_uq], writes=[s_vt[gl][s_]], dma=True)
            if mt + 1 < 8:
                load_params(mt + 1)
                load_w(win_v, 3 * D + (mt + 1) * 128, 0)
                load_w(win_v, 4 * D + (mt + 1) * 128, 1)

            f2 = lambda nm: gen[nm][0].rearrange("p a b -> p (a b)")
            sG = lambda nm: GEN[nm][1]
            items = [(gl, d) for gl in range(8) for d in range(2)]

            h16 = lambda ap: ap.bitcast(BF16)[:, 0:NF]

            def stage_T(k, part=0):
                gl, d = items[k]; e_ = d * 8 + gl; T_ = TS[k % 2]
                tu, s_u = T_["u"]; tfr, s_fr = T_["fr"]; SINt, s_sin = T_["SIN"]; COSt, s_cos = T_["COS"]
                SINt = h16(SINt); COSt = h16(COSt)
                fcol = pm[:, R_FCH, e_:e_ + 1]
                tui = tu.bitcast(I32)
                if part in (0, 1):
                    PA(lambda e: e.activation(out=tui, in_=idx2[:, d, :], func=AF.Identity, scale=fcol),
                       [s_idx, s_pm[R_FCH]], [s_u])
                if part == 1:
                    return
                PV_(lambda e: e.scalar_tensor_tensor(out=tfr, in0=idx2[:, d, :], scalar=fcol, in1=tui,
                                                     op0=ALU.mult, op1=ALU.subtract), [s_idx, s_pm[R_FCH], s_u], [s_fr])
                PA(lambda e: e.activation(out=tu, in_=tfr, func=AF.Abs), [s_fr], [s_u])
                PA(lambda e: e.activation(out=SINt, in_=tfr, func=AF.Sin, scale=TWO_PI), [s_fr], [s_sin])
                PA(lambda e: e.activation(out=COSt, in_=tu, func=AF.Sin, scale=-TWO_PI, bias=math.pi / 2), [s_u], [s_cos])

            def stage_G(k):
                gl, d = items[k]; e_ = d * 8 + gl
                Gm = lambda nm: GEN[nm][0][:, e_, :]
                Ban, Bbn, MTn = "Ba%d" % d, "Bb%d" % d, "MT%d" % d
                P.add("tensor", lambda e: e.transpose(out=psBb[d][:, 0:128], in_=Gm("BaT"), identity=idb[:]),
                      reads=[sG("BaT"), s_idb], writes=[s_psB[d]])
                P.add("tensor", lambda e: e.transpose(out=psBb[d][:, 128:256], in_=Gm("BbT"), identity=idb[:]),
                      reads=[sG("BbT"), s_idb], writes=[s_psB[d]])
                P.add("tensor", lambda e: e.matmul(psB[d][:, 256:384], lhsT=Gm("Q"), rhs=Gm("CT"), start=True, stop=True),
                      reads=[sG("Q"), sG("CT")], writes=[s_psB[d]])
                PA(lambda e: e.activation(out=rawap(f2(Ban), [[1, 128]]) if False else f2(Ban), in_=psBb[d][:, 0:128],
                                          func=AF.Copy), [s_psB[d]], [gen[Ban][1]])
                PA(lambda e: e.activation(out=f2(Bbn), in_=psBb[d][:, 128:256], func=AF.Copy), [s_psB[d]], [gen[Bbn][1]])
                PV_(lambda e: e.tensor_tensor(out=f2(MTn), in0=psB[d][:, 256:384], in1=msk[:, d, :], op=ALU.mult),
                    [s_psB[d], s_msk], [gen[MTn][1]])

            def stage_B(k):
                gl, d = items[k]; V = Vt[gl]
                Ban, Bbn = "Ba%d" % d, "Bb%d" % d
                col0 = 0 if d == 0 else CCH
                for (a0, a1) in pieces:
                    P.add("tensor", lambda e, a0=a0, a1=a1: e.matmul(
                        ZaP[:, a0:a1], lhsT=f2(Ban), rhs=V[:, col0 + a0:col0 + a1], start=True, stop=True),
                        reads=[gen[Ban][1]] + s_vt[gl], writes=[s_pstr[0]])
                    P.add("tensor", lambda e, a0=a0, a1=a1: e.matmul(
                        ZbP[:, a0:a1], lhsT=f2(Bbn), rhs=V[:, col0 + a0:col0 + a1], start=True, stop=True),
                        reads=[gen[Bbn][1]] + s_vt[gl], writes=[s_pstr[1]])

            def stage_M(k):
                T_ = TS[k % 2]
                SINt, s_sin = T_["SIN"]; COSt, s_cos = T_["COS"]; t1, s_t1 = T_["t1"]; t2, s_t2 = T_["t2"]
                SINt = h16(SINt); COSt = h16(COSt); t1 = h16(t1); t2 = h16(t2)
                for (a0, a1) in pieces:
                    PV_(lambda e, a0=a0, a1=a1: e.tensor_tensor(out=t1[:, a0:a1], in0=ZaP[:, a0:a1], in1=COSt[:, a0:a1],
                                                               op=ALU.mult), [s_pstr[0], s_cos], [s_t1])
                    PV_(lambda e, a0=a0, a1=a1: e.tensor_tensor(out=t2[:, a0:a1], in0=ZbP[:, a0:a1], in1=SINt[:, a0:a1],
                                                               op=ALU.mult), [s_pstr[1], s_sin], [s_t2])
                PV_(lambda e: e.tensor_tensor(out=t1, in0=t1, in1=t2, op=ALU.add), [s_t1, s_t2], [s_t1])

            def stage_S(k):
                gl, d = items[k]; e_ = d * 8 + gl; T_ = TS[k % 2]
                SINt, s_sin = T_["SIN"]; COSt, s_cos = T_["COS"]; t1, s_t1 = T_["t1"]; G, s_G = T_["u"]
                SINt = h16(SINt); COSt = h16(COSt); t1 = h16(t1); G = h16(G)
                Gc_, s_gc_ = T_["Gc"]; Gs_, s_gs_ = T_["Gs"]
                rho_b = bass.AP(pm.tensor, pm[:, R_RHO, e_:e_ + 1].offset, [list(pm.ap[0]), [0, NF]])
                if d == 0:
                    PV_(lambda e: e.tensor_tensor_scan(out=G, data0=rho_b, data1=t1, initial=0.0, op0=ALU.mult, op1=ALU.add),
                        [s_t1, s_pm[R_RHO]], [s_G])
                else:
                    PV_(lambda e: e.tensor_tensor_scan(out=rev(G, NF), data0=rho_b, data1=rev(t1, NF), initial=0.0,
                                                       op0=ALU.mult, op1=ALU.add), [s_t1, s_pm[R_RHO]], [s_G])
                PV_(lambda e: e.tensor_tensor(out=Gc_, in0=G, in1=COSt, op=ALU.mult), [s_G, s_cos], [s_gc_])
                PV_(lambda e: e.tensor_tensor(out=Gs_, in0=G, in1=SINt, op=ALU.mult), [s_G, s_sin], [s_gs_])

            def stage_Y(k):
                gl, d = items[k]; e_ = d * 8 + gl; T_ = TS[k % 2]
                g = 8 * mt + gl; yb = gl % 2; V = Vt[gl]
                Gm = lambda nm: GEN[nm][0][:, e_, :]
                MTn = "MT%d" % d
                Gc_, s_gc_ = T_["Gc"]; Gs_, s_gs_ = T_["Gs"]
                if d == 0:
                    dgm, s_dgm = gen["dgm"]
                    PV_(lambda e: e.tensor_scalar(out=dgm.rearrange("p a b -> p (a b)"), in0=idf[:],
                                                  scalar1=dcol[:, g:g + 1], scalar2=None, op0=ALU.mult),
                        [s_idf, s_dcol], [s_dgm])
                    P.add("tensor", lambda e: e.matmul(psA[yb][:, 0:NCH], lhsT=dgm.rearrange("p a b -> p (a b)"),
                                                       rhs=V[:, CCH:CCH + NCH], start=True, stop=False),
                          reads=[s_dgm] + s_vt[gl], writes=[s_psA[yb]])
                pc = CCH - 1 if d == 0 else 1
                P.add("tensor", lambda e: e.matmul(psA[yb][:, 0:NCH], lhsT=f2(MTn), rhs=V[:, CCH:CCH + NCH],
                                                   start=False, stop=False),
                      reads=[gen[MTn][1]] + s_vt[gl], writes=[s_psA[yb]])
                P.add("tensor", lambda e: e.matmul(psA[yb][:, 0:NCH], lhsT=Gm("CT"), rhs=Gc_[:, pc:pc + NCH],
                                                   start=False, stop=False),
                      reads=[sG("CT"), s_gc_], writes=[s_psA[yb]])
                P.add("tensor", lambda e: e.matmul(psA[yb][:, 0:NCH], lhsT=Gm("Cb"), rhs=Gs_[:, pc:pc + NCH],
                                                   start=False, stop=(d == 1)),
                      reads=[sG("Cb"), s_gs_], writes=[s_psA[yb]])

            def stage_E(k):
                gl, d = items[k]; T_ = TS[k % 2]
                yb = gl % 2
                Gc_, s_gc_ = T_["Gc"]
                if d == 1:
                    y2, s_y2 = junk[:, 0:NCH], s_junk
                    z_, s_z = junk[:, 512:512 + NCH], s_gz
                    Yp = psA[yb][:, 0:NCH]
                    PA(lambda e: e.activation(out=y2, in_=Yp, func=AF.Square, scale=math.sqrt(0.044715)), [s_psA[yb]], [s_y2])
                    PV_(lambda e: e.scalar_tensor_tensor(out=z_, in0=y2, scalar=1.0, in1=Yp, op0=ALU.add, op1=ALU.mult),
                        [s_psA[yb], s_y2], [s_z])
                    hy, s_hy = ghy[:, 0:NCH], s_ghy
                    PA(lambda e: e.activation(out=z_, in_=z_, func=AF.Tanh, scale=0.7978845608028654), [s_z], [s_z])
                    PA(lambda e: e.activation(out=hy, in_=Yp, func=AF.Copy, scale=0.5), [s_psA[yb]], [s_hy])
                    PV_(lambda e: e.scalar_tensor_tensor(out=Yq[yb], in0=z_, scalar=1.0, in1=hy, op0=ALU.add, op1=ALU.mult),
                        [s_z, s_hy], [s_yq[yb]])
                    for s_ in range(8):
                        P.add("gpsimd" if s_ % 2 == 0 else "sync", lambda e, s_=s_, mt=mt: e.dma_start(
                            out=gy_scr[mt, 16 * gl:16 * gl + 16, s_ * NCH:(s_ + 1) * NCH], in_=Yq[yb][16 * s_:16 * s_ + 16, :]),
                            reads=[s_yq[yb]], writes=[s_gyscr[mt][gl][s_]], dma=True)

            stage_T(0); stage_G(0); stage_B(0)
            lc = []
            for k in range(16):
                la = []
                if k + 1 < 16:
                    P.capture(); stage_T(k + 1, 1); stage_G(k + 1); stage_T(k + 1, 2); la = P.end_capture()
                P.capture(); stage_M(k); lb = P.end_capture()
                P.add_merged(la, lb, lc)
                if k + 1 < 16:
                    stage_B(k + 1)
                stage_S(k)
                stage_Y(k)
                P.capture(); stage_E(k); lc = P.end_capture()
            P.add_merged(lc)
        if "gy" in dbg:
            dd = dram("dbg_gy", [8, 128, L], BF16, "ExternalOutput")
            gyt, s_gyt = carve([8, L], BF16, "gyt") if 16 * L < 65536 else (None, None)
            for mt in range(8):
                P.add("sync", lambda e, mt=mt: e.dma_start(out=gyt[:, mt, :], in_=gy_scr[mt, :, :]),
                      reads=[y_ for x in s_gyscr[mt] for y_ in x], writes=[s_gyt], dma=True)
            stores.append(P.add("sync", lambda e: e.dma_start(out=dd.rearrange("m p t -> p m t"), in_=gyt),
                                reads=[s_gyt], dma=True))
        if stage == 4:
            P.add("sync", lambda e: e.nop(), reads=[], writes=[], force=True).deps.update(o.idx for o in stores if o.idx >= 0)
            P.emit(nc)
            es.close()
            return nc

    wout_d = dram("w_out", [2 * D, D])
    fg_d = dram("fg_b", [128, D])
    fg = xn_all[:].rearrange("p a d -> p (a d)").bitcast(F32); s_fg = s_xn[0]
    P.add("sync", lambda e: e.dma_start(out=fg, in_=fg_d[:, :]), writes=[s_xn[0], s_xn[1]], dma=True)
    wout_v = wout_d.rearrange("(kt p) n -> p kt n", p=128)

    def load_wo(m, slot):
        for half in range(2):
            st = wcnt[0] % 2
            wcnt[0] += 1
            P.add("sync", lambda e, st=st, half=half: e.dma_start(
                out=wst[st][:], in_=wout_v[:, half * 8:(half + 1) * 8, m * 128:(m + 1) * 128]),
                writes=[s_wst[st]], dma=True)
            P.add("gpsimd", lambda e, st=st, half=half: e.tensor_copy(
                out=wom[slot][:, half * 8:(half + 1) * 8, :], in_=wst[st][:]),
                reads=[s_wst[st]], writes=[s_wom[slot]])
    cin1 = dg[:].rearrange("p t c -> p (t c)").rearrange("p (j t) -> p j t", t=512)
    mixg = vpad[:].bitcast(F32).rearrange("p (m t) -> p m t", t=512); s_mixg = s_v
    xres = [xt[0], xt[1]]; s_xres = [s_xt[0], s_xt[1]]
    xo = [xt[2], sm[:, 0:4, :].rearrange("p a t -> p (a t)").bitcast(F32)]
    s_xo = [[s_xt[2]], [s_sgt[0], s_sgt[1], s_sqt[0], s_sqt[1]]]
    psT = [ps_tr[i][:].rearrange("p k t -> p (k t)").bitcast(F32) for i in range(2)]
    x_sc = x_d.rearrange("(c s) d -> s c d", s=8)
    y_sc = y_d.rearrange("(c s) d -> s c d", s=8)
    K_TILES = 8 if stage < 5 else 16
    if stage >= 5:
        glu_d = dram("glu_w", [D, D])
        glu_v = glu_d.rearrange("(kt p) n -> p kt n", p=128)
        fence2 = P.add("gpsimd", lambda e: e.memset(fz[:], 0.0), writes=arena_slots + old_slots)
        aoffs["main"] = 0
        aoff[0] = 0
        _c = carve
        gin, s_gin = _c([8, 512], BF16, "gin"); s_gin.last_w = fence2.idx
        sso, s_sso = _c([8, 512], BF16, "sso"); s_sso.last_w = fence2.idx
        sgx = sb("sgx", [128, 2, 512], BF16)
        sg2, s_sg2 = sgx[:, 0, :], S("sg2")
        sgate, s_sgate = sgx[:, 1, :], S("sgate")
        wo_r, s_wor = _c([8, 16, 128], BF16, "wo_r"); s_wor.last_w = fence2.idx
        wglu_r, s_wgr = _c([8, 8, 128], BF16, "wglu_r"); s_wgr.last_w = fence2.idx
        s_wor_l = [s_wor] + [Slot("wor%d" % i) for i in range(1, 8)]
        s_wgr_l = [s_wgr] + [Slot("wgr%d" % i) for i in range(1, 8)]
        for x_ in s_wor_l[1:] + s_wgr_l[1:]:
            x_.last_w = fence2.idx
        for m in range(8):
            for half in range(2):
                load_w(wout_v, m * 128, 0, dst=wo_r[:, m, half * 8:(half + 1) * 8, :], dslots=[s_wor_l[m]], k0=half * 8,
                       ceng=("vector", "scalar")[half])
        for m2 in range(8):
            load_w(glu_v, m2 * 128, 0, dst=wglu_r[:, m2, :, :], dslots=[s_wgr_l[m2]], ceng=("vector", "scalar")[m2 % 2])
    def load_gin(q):
        sl = slice(q * 512, (q + 1) * 512)
        P.add("sync", lambda e, sl=sl: e.dma_start(out=gin, in_=gy_scr[:, :, sl].rearrange("m p t -> p m t")),
              reads=[y_ for r in s_gyscr for x in r for y_ in x], writes=[s_gin], dma=True)
        for hf in range(2):
            P.add("sync", lambda e, sl=sl, hf=hf: e.dma_start(
                out=wom[hf][:, 0:4, :].rearrange("p m (a t) -> p (m a) t", t=512) if False else sgin[hf],
                in_=sg_scr[4 * hf:4 * hf + 4, :, sl].rearrange("m p t -> p m t")),
                reads=s_sgscr, writes=[s_wom[hf]], dma=True)
    if stage >= 5:
        sgin = [wom[hf][:].rearrange("p k c -> p (k c)").rearrange("p (m t) -> p m t", t=512) for hf in range(2)]
        hb = sb("hb", [128, 8])
        s_hb = S("hb")
        P.add("vector", lambda e: e.tensor_scalar(out=hb[:], in0=pv[:, PV_GLB:PV_GLB + 8], scalar1=0.5, scalar2=None,
                                                  op0=ALU.mult), reads=[s_pv], writes=[s_hb])
        load_gin(0)
    def glu_part(q):
        cb = q % 2
        if stage >= 5:
            for m2 in range(8):
                b = m2 % 2
                P.add("tensor", lambda e, b=b, m2=m2: [e.matmul(psB[b][:, :], lhsT=wglu_r[:, m2, k, :], rhs=gin[:, k, :],
                                                                start=(k == 0), stop=(k == 7)) for k in range(8)][-1],
                      reads=[s_wgr_l[m2], s_gin], writes=[s_psB[b]])
                P.add("scalar", lambda e, b=b, m2=m2: e.activation(out=sg2, in_=psB[b][:], func=AF.Tanh, scale=0.5,
                                                                 bias=hb[:, m2:m2 + 1]),
                      reads=[s_psB[b], s_hb], writes=[s_sg2])
                P.add("vector", lambda e, m2=m2: e.scalar_tensor_tensor(out=sso[:, m2, :], in0=sg2, scalar=1.0,
                                                                        in1=gin[:, m2, :], op0=ALU.add, op1=ALU.mult),
                      reads=[s_gin, s_sg2], writes=[s_sso])
                P.add("vector", lambda e, m2=m2: e.scalar_tensor_tensor(out=sso[:, m2, :], in0=sso[:, m2, :], scalar=0.5,
                                                                        in1=sgin[m2 // 4][:, m2 % 4, :], op0=ALU.mult,
                                                                        op1=ALU.mult),
                      reads=[s_sso, s_wom[m2 // 4]], writes=[s_sso])
    def load_cin(q):
        cb = q % 2
        P.add("sync", lambda e, q=q: e.dma_start(
            out=cin1, in_=conv_scr[:, :, q * 512:(q + 1) * 512].rearrange("j p t -> p j t")),
            reads=[s_scr[j][q] for j in range(8)], writes=[s_dg], dma=True)
    def outproj(q):
        cb = q % 2
        for m in range(8):
            b = m % 2

            if stage < 5:
                load_wo(m, b)

            def mo(e, m=m, b=b, cb=cb, q=q):
                ins = None
                for k in range(K_TILES):
                    rhs = cin1[:, k, :] if k < 8 else sso[:, k - 8, :]
                    ins = e.matmul(psA[b][:, :], lhsT=(wom[b][:, k, :] if stage < 5 else wo_r[:, m, k, :]), rhs=rhs,
                                   start=(k == 0), stop=(k == K_TILES - 1))
                return ins
            rd = ([s_wom[b], s_dg] if stage < 5 else [s_wor_l[m], s_dg, s_sso])
            P.add("tensor", mo, reads=rd, writes=[s_psA[b]])
            P.add("vector", lambda e, m=m, b=b: e.tensor_scalar(out=mixg[:, m, :], in0=psA[b][:], scalar1=ada[:, 16 + m, 0:1],
                                                               scalar2=None, op0=ALU.mult),
                  reads=[s_psA[b], s_ada], writes=[s_mixg])
    def tail_part(q):
        cb = q % 2
        for pb in range(4):
            gi = q * 4 + pb
            tb = gi % 2
            p0 = q * 512 + pb * 128
            s_i, c0 = p0 // NCH, p0 % NCH
            ncb = min(128, NCH)
            runs = [(s_i + r, c0 if NCH >= 128 else 0, ncb) for r in range(128 // ncb)]
            for ri, (ss, cc, n) in enumerate(runs):
                P.add("gpsimd", lambda e, tb=tb, ss=ss, cc=cc, n=n, ri=ri: e.dma_start(
                    out=xres[tb][ri * n:(ri + 1) * n, :], in_=x_sc[ss, cc:cc + n, :]),
                    writes=[s_xres[tb]], dma=True)

            def trf(e, tb=tb, pb=pb):
                ins = None
                for m in range(8):
                    ins = e.transpose(out=psT[tb][:, m * 128:(m + 1) * 128], in_=mixg[:, m, pb * 128:(pb + 1) * 128],
                                      identity=idf[:])
                return ins
            P.add("tensor", trf, reads=[s_mixg, s_idf], writes=[s_pstr[tb]])
            P.add("vector", lambda e, tb=tb: e.tensor_tensor(out=xo[tb], in0=psT[tb], in1=xres[tb][:], op=ALU.add),
                  reads=[s_pstr[tb], s_xres[tb]], writes=s_xo[tb])
            P.add("scalar", lambda e, tb=tb: e.activation(out=junk[:], in_=xo[tb], func=AF.Square,
                                                         accum_out=ssq[tb][:]),
                  reads=s_xo[tb], writes=[s_junk, s_ssq[tb]])
            P.add("scalar", lambda e, tb=tb: e.activation(out=rstd[tb][:], in_=ssq[tb][:], func=AF.Sqrt,
                                                         scale=1.0 / D, bias=EPS),
                  reads=[s_ssq[tb]], writes=[s_rstd[tb]])
            P.add("vector", lambda e, tb=tb: e.reciprocal(out=rstd[tb][:], in_=rstd[tb][:]),
                  reads=[s_rstd[tb]], writes=[s_rstd[tb]])
            P.add("vector", lambda e, tb=tb: e.scalar_tensor_tensor(out=xo[tb], in0=xo[tb], scalar=rstd[tb][:, 0:1],
                                                                  in1=fg, op0=ALU.mult, op1=ALU.mult),
                  reads=s_xo[tb] + [s_rstd[tb], s_xn[0], s_xn[1]], writes=s_xo[tb])
            for ri, (ss, cc, n) in enumerate(runs):
                stores.append(P.add("sync", lambda e, tb=tb, ss=ss, cc=cc, n=n, ri=ri: e.dma_start(
                    out=y_sc[ss, cc:cc + n, :], in_=xo[tb][ri * n:(ri + 1) * n, :]),
                    reads=s_xo[tb], dma=True))
    load_cin(0)
    if stage >= 5:
        glu_part(0)
    for q in range(NT):
        if stage >= 5 and q + 1 < NT:
            load_gin(q + 1)
        outproj(q)
        if q + 1 < NT:
            load_cin(q + 1)
            if stage >= 5:
                glu_part(q + 1)
        tail_part(q)
    P.add("sync", lambda e: e.nop(), reads=[], writes=[], force=True).deps.update(o.idx for o in stores if o.idx >= 0)
    P.emit(nc)
    es.close()
    return nc


def _col8(v):
    return np.ascontiguousarray(np.asarray(v, np.float32).reshape(-1, 128).T)


def prep_shared(inp, L_=4096, CTX_=256):
    f = lambda k: np.asarray(inp[k], np.float32)
    sh = {}
    sh["ident"] = np.eye(128, dtype=np.float32)
    sh["w_ada"] = np.ascontiguousarray(f("w_ada")[0])
    sh["w_in"] = np.ascontiguousarray(f("w_in")[0])
    sh["w_out"] = np.ascontiguousarray(f("w_out")[0])
    sh["glu_w"] = np.ascontiguousarray(f("ssm_glu_w")[0])
    NF_ = (L_ + CTX_) // 8
    kk = np.zeros((16, 25), np.float32)
    sidx = np.arange(8, dtype=np.float32)
    for e in range(16):
        if e < 8:
            kk[e, 0:8] = 7 - sidx; kk[e, 8:16] = -(sidx + 1); kk[e, 16:24] = sidx + 1
        else:
            kk[e, 0:8] = sidx; kk[e, 8:16] = sidx - 8; kk[e, 16:24] = 8 - sidx
        kk[e, 24] = 1.0
    sh["kk"] = np.ascontiguousarray(np.broadcast_to(kk.reshape(1, 400), (128, 400)))
    sg = np.ones((128, 2), np.float32); sg[64:, 0] = -1.0; sg[:, 1] = -sg[:, 0]
    sh["sgn"] = sg
    ix = np.arange(NF_, dtype=np.float32)
    sh["idx2"] = np.ascontiguousarray(np.broadcast_to(np.concatenate([ix, ix[::-1]])[None, :], (128, 2 * NF_)))
    sp = np.repeat(np.arange(8), 16)
    mf = (sp[None, :] >= sp[:, None]).astype(np.float32)
    sh["msk"] = np.ascontiguousarray(np.concatenate([mf, mf.T], axis=1))
    dd = f("ssm_d")[0].reshape(64, 16)
    sh["dcol"] = np.ascontiguousarray(np.tile(dd.T, (8, 1)))
    n2 = np.arange(128) % 64
    top = (np.arange(128) < 64)
    are = f("ssm_a_re")[0]; aim = f("ssm_a_im")[0]; ldt = f("ssm_log_dt")[0]
    bre = f("ssm_b_re")[0]; bim = f("ssm_b_im")[0]
    cre = f("ssm_c_re")[0]; cim = f("ssm_c_im")[0]
    sc = np.zeros((8, 128, 48), np.float32)
    bx = np.zeros((8, 128, 2, 16, 16), np.float32)
    cx = np.zeros((8, 128, 2, 16, 16), np.float32)
    for mt in range(8):
        for d in range(2):
            for gl in range(8):
                g = 8 * mt + gl; e = d * 8 + gl
                sc[mt, :, e] = are[d, g, n2]
                sc[mt, :, 16 + e] = aim[d, g, n2]
                sc[mt, :, 32 + e] = ldt[d, g]
                br = bre[d, g][n2, :]; bi = bim[d, g][n2, :]
                bx[mt, :, 0, e, :] = np.where(top[:, None], br, bi)
                bx[mt, :, 1, e, :] = np.where(top[:, None], bi, br)
                cr = cre[d, g].T[n2, :]; ci = cim[d, g].T[n2, :]
                cx[mt, :, 0, e, :] = np.where(top[:, None], cr, ci)
                cx[mt, :, 1, e, :] = np.where(top[:, None], ci, cr)
    sh["ssm_sc"] = sc
    sh["ssm_bx"] = bx.reshape(8, 128, 512)
    sh["ssm_cx"] = cx.reshape(8, 128, 512)
    sh["fg_b"] = np.ascontiguousarray(np.broadcast_to(f("final_g")[None, :], (128, D)))
    dw = f("conv_dw")[0]
    dwl = dw.T.reshape(8, 128, 31).transpose(1, 0, 2).reshape(128, 248)
    sh["_pv_tail"] = np.concatenate([
        _col8(f("c_ctx")), _col8(f("norm_g")[0]), _col8(f("conv_db")[0]), _col8(f("conv_ln_g")[0]),
        _col8(f("conv_ln_b")[0]), _col8(f("ssm_d")[0]), _col8(f("ssm_glu_b")[0]), _col8(f("b_ada")[0]), dwl], axis=1)
    return sh


def prep_core(inp, b, L, CTX, sh=None):
    if sh is None:
        sh = prep_shared(inp, L, CTX)
    m = {k: v for k, v in sh.items() if not k.startswith("_")}
    m["x"] = np.ascontiguousarray(np.asarray(inp["x"], np.float32)[b, :L])
    m["ctx"] = np.ascontiguousarray(np.asarray(inp["ctx"], np.float32)[b, :CTX])
    m["pv"] = np.ascontiguousarray(np.concatenate([_col8(np.asarray(inp["c"], np.float32)[b]), sh["_pv_tail"]], axis=1))
    return m


def kernel(**inputs):
    L, CTX, NB = 4096, 256, 8
    sh = prep_shared(inputs, L, CTX)
    in_maps = [prep_core(inputs, b, L, CTX, sh) for b in range(NB)]
    nc = bass.Bass("TRN2", target_bir_lowering=False)
    build(nc, L, CTX, stage=5)
    res = run_bass_kernel_spmd(nc, in_maps, core_ids=list(range(NB)))
    return np.stack([np.asarray(r["y"], np.float32) for r in res.results], axis=0)
```
